# Optimizing a Trainium2 kernel written in Bass

```python
import math
import jax, jax.numpy as jnp
from jax import lax
import numpy as np

D_MODEL = 1024
BATCH = 8
SEQ = 2048
DEPTH = 2
DEC_BATCH = 128
DEC_SEQ = 1
PAST_LEN = 16384
PAGE_SIZE = 128

H_RET = 4
RET_DK = 128
RET_DV = 128
RET_W = H_RET * RET_DV
RET_CHUNK = 128
GM_GROUPS = 4
GM_CH = 128
GM_W = GM_GROUPS * GM_CH
GMLP_CHUNK = 128
MIX_W = RET_W + GM_W
IN_W = 4 * RET_W + 2 * GM_W
D_FF = 2048
CONV_W = 3
ROPE_BASE = 10000.0
EPS = 1e-6

kernel_name = "hybrid_retention_gmlp_convffn_step"


def rmsnorm(x, g):
    xf = x.astype(jnp.float32)
    r = lax.rsqrt(jnp.mean(xf * xf, axis=-1, keepdims=True) + EPS)
    return (xf * r).astype(x.dtype) * g


def modulate(h, shift, scale):
    return h * (1.0 + scale[:, None, :]) + shift[:, None, :]


def rope(x, pos):
    half = x.shape[-1] // 2
    freqs = jnp.exp(-math.log(ROPE_BASE) * jnp.arange(half, dtype=jnp.float32) / half)
    ang = pos.astype(jnp.float32)[:, None] * freqs[None, :]
    cos = jnp.cos(ang)[None, :, None, :]
    sin = jnp.sin(ang)[None, :, None, :]
    xf = x.astype(jnp.float32)
    x1, x2 = xf[..., :half], xf[..., half:]
    return jnp.concatenate([x1 * cos - x2 * sin, x1 * sin + x2 * cos], axis=-1)


def retention(q, k, v, s0):
    B, L, H, _ = q.shape
    C = RET_CHUNK if L % RET_CHUNK == 0 else L
    n = L // C
    lg = jnp.log1p(-jnp.exp2(-5.0 - jnp.arange(H, dtype=jnp.float32)))
    i = jnp.arange(C, dtype=jnp.float32)
    diff = i[:, None] - i[None, :]
    dmask = jnp.where(diff[None] >= 0.0,
                      jnp.exp(jnp.maximum(diff, 0.0)[None] * lg[:, None, None]), 0.0)
    q_dec = jnp.exp((i[:, None] + 1.0) * lg[None, :])
    k_dec = jnp.exp((C - 1.0 - i)[:, None] * lg[None, :])
    chunk_dec = jnp.exp(C * lg)

    def to_chunks(t):
        return t.reshape(B, n, C, H, t.shape[-1]).swapaxes(0, 1)

    def step(S, inp):
        qc, kc, vc = inp
        att = jnp.einsum('bihd,bjhd->bhij', qc, kc) * dmask[None]
        o = (jnp.einsum('bhij,bjhe->bihe', att, vc)
             + jnp.einsum('bihd,bhde->bihe', qc, S) * q_dec[None, :, :, None])
        S = (chunk_dec[None, :, None, None] * S
             + jnp.einsum('bjhd,bjhe->bhde', kc * k_dec[None, :, :, None], vc))
        return S, o

    S, o = lax.scan(step, s0, (to_chunks(q), to_chunks(k), to_chunks(v)))
    return o.swapaxes(0, 1).reshape(B, L, H, -1), S


def chunk_gmlp(u, v, w_s, b_s, ln_g):
    B, L, _ = u.shape
    vg = v.reshape(B, L, GM_GROUPS, GM_CH).astype(jnp.float32)
    mu = jnp.mean(vg, axis=-1, keepdims=True)
    var = jnp.mean(jnp.square(vg - mu), axis=-1, keepdims=True)
    vn = (vg - mu) * lax.rsqrt(var + EPS) * ln_g.astype(jnp.float32).reshape(GM_GROUPS, GM_CH)
    C = GMLP_CHUNK if L % GMLP_CHUNK == 0 else L
    n = L // C
    mask = jnp.tril(jnp.ones((C, C), dtype=bool))
    W = jnp.where(mask[None], w_s[:, :C, :C].astype(jnp.float32), 0.0)
    s = (jnp.einsum('gij,bnjgc->bnigc', W, vn.reshape(B, n, C, GM_GROUPS, GM_CH))
         + b_s[:, :C].astype(jnp.float32).T[None, None, :, :, None])
    out = u.astype(jnp.float32) * s.reshape(B, L, GM_W)
    return out.astype(u.dtype), vn.reshape(B, L, GM_W).astype(u.dtype)


def causal_dwconv(a, buf, w, b):
    L = a.shape[1]
    full = jnp.concatenate([buf.astype(a.dtype), a], axis=1)
    out = b + sum(full[:, t:t + L, :] * w[t] for t in range(CONV_W))
    return out, full[:, -(CONV_W - 1):, :]


def layer(x, c, pos, s_ret, conv_buf, w_ada, b_ada, g_mix, w_in, ret_gn_gain, gmlp_ln_gain,
          w_s, b_s, w_out, g_ffn, w_up, conv_w, conv_b, w_down):
    B, L, _ = x.shape
    mod = jax.nn.silu(c) @ w_ada + b_ada
    sh1, sc1, gt1, sh2, sc2, gt2 = jnp.split(mod, 6, axis=-1)

    h = modulate(rmsnorm(x, g_mix), sh1, sc1)
    p = h @ w_in
    q, k, v, g, u, vv = jnp.split(p, [RET_W, 2 * RET_W, 3 * RET_W, 4 * RET_W, 4 * RET_W + GM_W], axis=-1)
    q = rope(q.reshape(B, L, H_RET, RET_DK), pos)
    k = rope(k.reshape(B, L, H_RET, RET_DK), pos) * (RET_DK ** -0.5)
    v = v.reshape(B, L, H_RET, RET_DV).astype(jnp.float32)
    o, s_new = retention(q, k, v, s_ret.astype(jnp.float32))
    mu = jnp.mean(o, axis=-1, keepdims=True)
    var = jnp.mean(jnp.square(o - mu), axis=-1, keepdims=True)
    o = (o - mu) * lax.rsqrt(var + EPS) * ret_gn_gain.astype(jnp.float32).reshape(H_RET, RET_DV)
    o_ret = (jax.nn.silu(g.astype(jnp.float32)) * o.reshape(B, L, RET_W)).astype(x.dtype)
    o_gm, v_rows = chunk_gmlp(jax.nn.gelu(u), jax.nn.gelu(vv), w_s, b_s, gmlp_ln_gain)
    mix = jnp.concatenate([o_ret, o_gm], axis=-1) @ w_out
    x = x + gt1[:, None, :] * mix

    h2 = modulate(rmsnorm(x, g_ffn), sh2, sc2)
    a, buf_new = causal_dwconv(h2 @ w_up, conv_buf, conv_w, conv_b)
    f = jax.nn.silu(a[..., :D_FF]) * a[..., D_FF:]
    x = x + gt2[:, None, :] * (f @ w_down)
    return x, s_new.astype(s_ret.dtype), buf_new, v_rows


def setup_inputs(seed: int = 0) -> dict:
    key = jax.random.key(seed)
    ks = jax.random.split(key, 24)

    def nrm(k, shape, s):
        return jax.random.normal(k, shape, jnp.float32) * s

    D = D_MODEL
    return {
        "x_prompt": nrm(ks[0], (BATCH, SEQ, D), 1.0),
        "x_sample": nrm(ks[1], (DEC_BATCH, DEC_SEQ, D), 1.0),
        "state_ret": nrm(ks[2], (DEPTH, DEC_BATCH, H_RET, RET_DK, RET_DV), 0.5),
        "state_conv": nrm(ks[3], (DEPTH, DEC_BATCH, CONV_W - 1, 2 * D_FF), 1.0),
        "c_prompt": nrm(ks[4], (BATCH, D), 1.0),
        "c_sample": nrm(ks[5], (DEC_BATCH, D), 1.0),
        "w_ada": nrm(ks[6], (DEPTH, D, 6 * D), D ** -0.5),
        "b_ada": nrm(ks[7], (DEPTH, 6 * D), 0.01),
        "g_mix": 1.0 + nrm(ks[8], (DEPTH, D), 0.01),
        "w_in": nrm(ks[9], (DEPTH, D, IN_W), D ** -0.5),
        "ret_gn_gain": 1.0 + nrm(ks[10], (DEPTH, RET_W), 0.01),
        "gmlp_ln_gain": 1.0 + nrm(ks[11], (DEPTH, GM_W), 0.01),
        "w_s": nrm(ks[12], (DEPTH, GM_GROUPS, GMLP_CHUNK, GMLP_CHUNK), GMLP_CHUNK ** -0.5),
        "b_s": 1.0 + nrm(ks[13], (DEPTH, GM_GROUPS, GMLP_CHUNK), 0.1),
        "w_out": nrm(ks[14], (DEPTH, MIX_W, D), MIX_W ** -0.5),
        "g_ffn": 1.0 + nrm(ks[15], (DEPTH, D), 0.01),
        "w_up": nrm(ks[16], (DEPTH, D, 2 * D_FF), D ** -0.5),
        "conv_w": nrm(ks[17], (DEPTH, CONV_W, 2 * D_FF), CONV_W ** -0.5),
        "conv_b": nrm(ks[18], (DEPTH, 2 * D_FF), 0.01),
        "w_down": nrm(ks[19], (DEPTH, D_FF, D), D_FF ** -0.5),
        "g_final": 1.0 + nrm(ks[20], (D,), 0.01),
    }


def reference(x_prompt, x_sample, state_ret, state_conv, c_prompt, c_sample, w_ada, b_ada, g_mix,
              w_in, ret_gn_gain, gmlp_ln_gain, w_s, b_s, w_out, g_ffn, w_up, conv_w, conv_b,
              w_down, g_final):
    pos_p = jnp.arange(SEQ, dtype=jnp.int32)
    pos_s = PAST_LEN + jnp.arange(DEC_SEQ, dtype=jnp.int32)
    nb = x_prompt.shape[0]
    xp, xs = x_prompt, x_sample
    ret_p, conv_p, ret_s, conv_s, v_s = [], [], [], [], []
    for l in range(DEPTH):
        params = (w_ada[l], b_ada[l], g_mix[l], w_in[l], ret_gn_gain[l], gmlp_ln_gain[l], w_s[l],
                  b_s[l], w_out[l], g_ffn[l], w_up[l], conv_w[l], conv_b[l], w_down[l])
        s0 = jnp.zeros((nb, H_RET, RET_DK, RET_DV), jnp.float32)
        b0 = jnp.zeros((nb, CONV_W - 1, 2 * D_FF), xp.dtype)
        xp, sp, bp, _ = layer(xp, c_prompt, pos_p, s0, b0, *params)
        xs, ss, bs, vs = layer(xs, c_sample, pos_s, state_ret[l], state_conv[l], *params)
        ret_p.append(sp); conv_p.append(bp); ret_s.append(ss); conv_s.append(bs); v_s.append(vs)
    y_prompt = rmsnorm(xp, g_final)
    y_sample = rmsnorm(xs, g_final)
    new_ret_prompt = jnp.stack(ret_p)
    new_conv_prompt = jnp.stack(conv_p)
    new_ret_sample = jnp.stack(ret_s)
    new_conv_sample = jnp.stack(conv_s)
    new_gmlp_v_sample = jnp.stack(v_s)
    return (y_prompt, y_sample, new_ret_prompt, new_conv_prompt, new_ret_sample, new_conv_sample, new_gmlp_v_sample)
```

```python
import contextlib
import math
import numpy as np
import ml_dtypes
import concourse.bass as bass
import concourse.mybir as mybir
from concourse.bass_utils import run_bass_kernel_spmd

F32 = mybir.dt.float32
BF16 = mybir.dt.bfloat16
AF = mybir.ActivationFunctionType
ALU = mybir.AluOpType
AX = mybir.AxisListType

ENGS = ("pe", "act", "dve", "pool", "sp")
NCORES = 8
T = 2048
NCH = 16
D = 1024
NS = 16
EPS = 1e-6
GAM = [1.0 - 2.0 ** (-5 - h) for h in range(4)]


class Res:
    __slots__ = ("name", "writer", "readers")

    def __init__(self, name):
        self.name = name
        self.writer = None
        self.readers = []


class Op:
    __slots__ = ("eng", "fn", "dma_key", "deps", "signal", "ticket", "idx")

    def __init__(self, eng, fn, dma_key, idx):
        self.eng = eng
        self.fn = fn
        self.dma_key = dma_key
        self.deps = []
        self.signal = False
        self.ticket = None
        self.idx = idx


class Prog:
    def __init__(self, nc):
        self.nc = nc
        self.ops = []
        self.res = {}
        self.dma_count = {}

    def R(self, name):
        r = self.res.get(name)
        if r is None:
            r = Res(name)
            self.res[name] = r
        return r

    def op(self, eng, fn, reads=(), writes=(), dma_key=None):
        o = Op(eng, fn, dma_key, len(self.ops))
        deps = {}
        is_dma = dma_key is not None

        def add(d):
            if d is None or d is o:
                return
            deps[d.idx] = d

        rs = [self.R(r) for r in reads]
        ws = [self.R(w) for w in writes]
        for r in rs:
            add(r.writer)
        for w in ws:
            pw = w.writer
            if pw is not None:
                same_eng_compute = (pw.eng == eng == "pe" and pw.dma_key is None and not is_dma)
                same_key_dma = (is_dma and pw.dma_key == dma_key)
                if not (same_eng_compute or same_key_dma):
                    add(pw)
            for rd in w.readers:
                if rd.eng == eng == "pe" and rd.dma_key is None and not is_dma:
                    continue
                add(rd)
        for r in rs:
            r.readers.append(o)
        for w in ws:
            w.writer = o
            w.readers = []
        if is_dma:
            self.dma_count[dma_key] = self.dma_count.get(dma_key, 0) + 1
        latest = {}
        for d in deps.values():
            k = ("dma", d.dma_key) if d.dma_key is not None else ("eng", d.eng)
            if k not in latest or latest[k].idx < d.idx:
                latest[k] = d
        for d in latest.values():
            if d.dma_key is not None:
                o.deps.append((d, self.dma_count[d.dma_key] * 16))
            else:
                o.deps.append((d, None))
                d.signal = True
        self.ops.append(o)
        return o

    def emit(self):
        nc = self.nc
        counters = {e: 0 for e in ENGS}
        for o in self.ops:
            if o.dma_key is None and o.signal:
                counters[o.eng] += 1
                o.ticket = counters[o.eng]
        keys = sorted(self.dma_count.keys())
        self.stats = {e: (sum(1 for o in self.ops if o.eng == e), counters[e]) for e in ENGS}
        with contextlib.ExitStack() as st:
            esem = {e: st.enter_context(nc.semaphore("s_" + e)) for e in ENGS if e != "sp"}
            dsem = {k: st.enter_context(nc.semaphore("d_" + str(k))) for k in keys}
            block = st.enter_context(nc.Block())
            per_eng = {e: [o for o in self.ops if o.eng == e] for e in ENGS}

            def run(engname, engobj):
                waited = {}
                for o in per_eng[engname]:
                    need = {}
                    for d, val in o.deps:
                        if d.dma_key is not None:
                            s = dsem[d.dma_key]
                            v = val
                        else:
                            s = esem[d.eng]
                            v = d.ticket
                        key = id(s)
                        if v > waited.get(key, 0) and v > need.get(key, (None, 0))[1]:
                            need[key] = (s, v)
                    for key, (s, v) in need.items():
                        engobj.wait_ge(s, v)
                        waited[key] = v
                    ins = o.fn(engobj)
                    if o.dma_key is not None:
                        ins.then_inc(dsem[o.dma_key], 16)
                    elif o.signal:
                        ins.then_inc(esem[o.eng], 1)
                if engname == "sp":
                    for k in keys:
                        engobj.wait_ge(dsem[k], self.dma_count[k] * 16)

            @block.sync
            def _(e):
                run("sp", e)

            @block.scalar
            def _(e):
                run("act", e)

            @block.vector
            def _(e):
                run("dve", e)

            @block.gpsimd
            def _(e):
                run("pool", e)

            @block.tensor
            def _(e):
                run("pe", e)


def build_nc():
    nc = bass.Bass("TRN2", target_bir_lowering=False)

    def din(name, shape, dt=F32):
        return nc.dram_tensor(name, list(shape), dt, kind="ExternalInput").ap()

    def dout(name, shape, dt=F32):
        return nc.dram_tensor(name, list(shape), dt, kind="ExternalOutput").ap()

    xp = din("xp", [T, D])
    xs = din("xs", [NS, D])
    cc = din("cc", [NS + 1, D])
    sret = din("sret", [2, NS, 4, 128, 128])
    sconvT = din("sconvT", [128, 2, 2, 32, NS])
    w_ada = din("w_ada", [2, D, 6 * D])
    w_in = din("w_in", [2, D, 3072])
    w_out = din("w_out", [2, D, D])
    w_up = din("w_up", [2, D, 4096])
    w_down = din("w_down", [2, 2048, D])
    baT = din("baT", [128, 2, 6, 8])
    gT = din("gT", [128, 5, 8])
    gfin = din("gfin", [1, D])
    gng = din("gng", [2, 512])
    lng = din("lng", [2, 512])
    wsT = din("wsT", [2, 128, 4, 128])
    bsT = din("bsT", [128, 2, 4])
    ws00 = din("ws00", [2, 4])
    bs0 = din("bs0", [2, 4])
    cwT = din("cwT", [128, 2, 32, 3])
    cbT = din("cbT", [128, 2, 32])
    identb_d = din("identb", [128, 128], BF16)
    identf_d = din("identf", [128, 128])
    ropep = din("ropep", [NCH, 128, 256])
    ropes = din("ropes", [NS, 256])
    dmaskT_d = din("dmaskT", [128, 4, 128])
    qdecT_d = din("qdecT", [128, 4, 128])
    kdec_d = din("kdec", [128, 4])
    trilT_d = din("trilT", [128, 128])
    delta16_d = din("delta16", [128, NS, NS])
    deltaK_d = din("deltaK", [NS, NS])

    yp = dout("yp", [T, D])
    ys = dout("ys", [NS, D])
    nrp = dout("nrp", [2, 128, 4, 128])
    ncpT = dout("ncpT", [128, 2, 32, 2])
    nrs = dout("nrs", [2, NS, 4, 128, 128])
    ncsT = dout("ncsT", [128, 2, 2, 32, NS])
    nvs = dout("nvs", [2, NS, 512])

    with contextlib.ExitStack() as st:
        def sb(name, shape, dt=F32):
            return st.enter_context(nc.sbuf_tensor(name, list(shape), dt))

        def psb(name):
            return st.enter_context(nc.psum_tensor(name, [128, 512], F32))

        P = Prog(nc)
        op = P.op

        xres = sb("xres", [128, NCH, D])
        arena = sb("arena", [128, 49152], BF16)
        xs_t = sb("xs_t", [128, D])
        xn = sb("xn", [128, D], BF16)
        tmp = sb("tmp", [128, D])
        hT2 = sb("hT2", [128, 8, 256], BF16)
        hT = hT2[:, :, 0:128]
        stt = sb("stt", [128, 64])
        PH = sb("PH", [128, 4624])
        hist = sb("hist", [128, 32, 2])
        modT = sb("modT", [128, 2, 6, 8, NS + 1])
        baT_t = sb("baT_t", [128, 2, 6, 8])
        gT_t = sb("gT_t", [128, 5, 8])
        GTp = sb("GTp", [128, D])
        cT = sb("cT", [128, 8, NS + 1], BF16)
        identb = sb("identb_t", [128, 128], BF16)
        identf = sb("identf_t", [128, 128])
        mh = sb("mh", [128, 4])
        kdec = sb("kdec_t", [128, 4])
        deltaK = sb("deltaK_t", [128, NS])
        wsS_b = sb("wsS_b", [128, 4, NS], BF16)
        bsT_t = sb("bsT_t", [128, 2, 4])
        w00 = sb("w00", [128, 4])
        b00 = sb("b00", [128, 4])
        cw = sb("cw", [128, 2, 32, 3])
        cb = sb("cb", [128, 2, 32])

        def phv(off, n, dt=F32, pat=None, **kw):
            a = PH[:, off:off + n]
            if dt == BF16:
                a = a.bitcast(BF16)
            if pat:
                a = a.rearrange(pat, **kw)
            return a

        ropeb = phv(0, 512, F32, "p (s c) -> p s c", s=2)
        ropes_t = phv(512, 256)
        dmaskT = phv(768, 512, F32, "p (h t) -> p h t", h=4)
        qdecT = phv(1280, 512, F32, "p (h t) -> p h t", h=4)
        trilT = phv(1792, 128)
        delta16 = phv(1920, 256, F32, "p (b m) -> p b m", b=NS)
        gn_tab = phv(2176, 512)
        ln_tab = phv(2688, 512)
        wsT_b = phv(3200, 256, BF16, "p (g t) -> p g t", g=4)
        S_f = phv(3456, 512, F32, "p (h t) -> p h t", h=4)
        S_b = phv(3968, 256, BF16, "p (h t) -> p h t", h=4)
        fT = phv(0, 2048, BF16, "p (j t) -> p j t", j=16)
        U = phv(2048, 1040, F32, "p (a s t) -> p a s t", a=2, s=2)
        tb = phv(3088, 1024, F32, "p (a s t) -> p a s t", a=2, s=2)
        sl = phv(4112, 512, F32, "p (a t) -> p a t", a=2)
        scT0 = phv(2048 + 520, 512, F32, "p (m b) -> p m b", m=32)
        scT1 = phv(3088 + 512, 512, F32, "p (m b) -> p m b", m=32)
        upTs = tmp[:, 0:512].rearrange("p (m b) -> p m b", m=32)
        PH1_NAMES = ["ropeb0", "ropeb1", "ropes_t", "dmaskT", "qdecT", "trilT", "delta16", "gn_tab", "ln_tab", "wsT_b", "S_f", "S_b", "q2T"]
        PH2_NAMES = ["fT", "U00", "U01", "U10", "U11", "tb00", "tb01", "tb10", "tb11", "sl0", "sl1"]

        bank = [psb("bank%d" % i) for i in range(8)]
        pT = bank[0][:, 0:512].bitcast(BF16).rearrange("p (k t) -> p k t", k=8)
        pT2 = bank[1][:, 0:512].bitcast(BF16).rearrange("p (k t) -> p k t", k=8)
        pTf = [bank[0], bank[1]]
        pp = bank[2:8]
        PPN = ["pp%d" % i for i in range(6)]

        def aview(off, n, dt, pat=None, **kw):
            a = arena[:, off:off + n]
            if dt == F32:
                a = a.bitcast(F32)
            if pat:
                a = a.rearrange(pat, **kw)
            return a

        wa = [aview(0, 8192, BF16, "p (k n) -> p k n", k=8), aview(8192, 8192, BF16, "p (k n) -> p k n", k=8)]
        Win = aview(0, 24576, BF16, "p (k n) -> p k n", k=8)
        Wout = aview(24576, 8192, BF16, "p (k n) -> p k n", k=8)
        Wup = aview(0, 32768, BF16, "p (k n) -> p k n", k=8)
        Wdn = aview(32768, 16384, BF16, "p (k n) -> p k n", k=16)
        AU = 32768
        qr = [aview(AU + 0, 512, BF16), aview(AU + 1536, 512, BF16)]
        kr = [aview(AU + 512, 512, BF16), aview(AU + 2048, 512, BF16)]
        vb = [aview(AU + 1024, 512, BF16), aview(AU + 2560, 512, BF16)]
        kk = aview(AU + 3072, 512, BF16)
        sg = [aview(AU + 3584, 1024, F32), aview(AU + 4608, 1024, F32)]
        gu = [aview(AU + 5632, 1024, F32), aview(AU + 6656, 1024, F32)]
        gv = aview(AU + 7680, 1024, F32)
        vn = [aview(AU + 8704, 512, BF16), aview(AU + 9216, 512, BF16)]
        t1 = aview(AU + 9728, 512, F32)
        t2 = aview(AU + 10240, 512, F32)
        t12 = aview(AU + 9728, 1024, F32)
        hx = aview(AU + 10752, 1024, F32)
        gA = aview(AU + 11776, 1024, F32)
        cen = aview(AU + 12800, 1024, F32)
        qkT = aview(AU + 13824, 1024, BF16, "p (k t) -> p k t", k=8)
        mixinT = qkT
        mixin = aview(AU + 14848, 1024, BF16)
        attTm = aview(AU + 15872, 512, BF16, "p (k t) -> p k t", k=4)
        q2T = phv(4224, 256, BF16, "p (k t) -> p k t", k=4)
        vnf = sg[1]
        Sg_f = aview(AU + 10752, 2048, F32, "p (b h e) -> p b h e", b=2, h=4)
        Sg_b = aview(AU + 6656, 1024, BF16, "p (b h e) -> p b h e", b=2, h=4)
        Km = aview(AU + 1536, 1024, BF16, "p (b n) -> p b n", b=2)
        o1 = cen
        o_s = gv
        Qm = aview(AU + 14848, 1024, BF16, "p (h b m) -> p h b m", h=4, b=NS)
        AU_NAMES = ["qr0", "kr0", "vb0", "qr1", "kr1", "vb1", "kk", "sg0", "sg1", "gu0", "gu1", "gv", "vn0", "vn1",
                    "t1", "t2", "hx", "gA", "cen", "qkT", "mixin", "attTm"]

        def ld(dst, src, name, eng="sp", key="c"):
            op(eng, lambda e: e.dma_start(out=dst, in_=src), writes=[name], dma_key=key)

        ld(identb[:], identb_d, "identb")
        ld(identf[:], identf_d, "identf")
        ld(tmp[0:NS + 1, :], cc, "tmp")
        ld(baT_t[:], baT, "baT_t")
        ld(gT_t[:], gT, "gT_t")
        ld(xs_t[0:NS, :], xs, "xs_t")
        ld(kdec[:], kdec_d, "kdec")
        ld(deltaK[0:NS, :], deltaK_d, "deltaK")
        ld(bsT_t[:], bsT, "bsT_t")
        ld(cw[:], cwT, "cw")
        ld(cb[:], cbT, "cb")
        op("pool", lambda e: e.memset(mh[:], -0.5), writes=["mh"])
        for q4 in range(4):
            op("act", lambda e, q4=q4: e.dma_start(
                out=xres[:, q4 * 4:(q4 + 1) * 4, :],
                in_=xp[q4 * 512:(q4 + 1) * 512, :].rearrange("(c p) f -> p c f", p=128)),
               writes=["x%d" % c for c in range(q4 * 4, q4 * 4 + 4)], dma_key="x%d" % q4)

        op("act", lambda e: e.activation(out=xn[0:NS + 1, :], in_=tmp[0:NS + 1, :], func=AF.Silu),
           reads=["tmp"], writes=["xn"])
        for k in range(8):
            op("pe", lambda e, k=k: e.transpose(out=pT[:, k, 0:NS + 1], in_=xn[0:NS + 1, k * 128:(k + 1) * 128],
                                                identity=identb[0:NS + 1, 0:NS + 1]),
               reads=["xn", "identb"], writes=["pT"])
        op("dve", lambda e: e.tensor_copy(out=cT[:], in_=pT[:, :, 0:NS + 1]), reads=["pT"], writes=["cT"])
        ji = 0
        for l in range(2):
            for v in range(6):
                b = ji % 2
                wname = "wa%d" % b
                op("pool", lambda e, l=l, v=v, b=b: e.dma_start(
                    out=wa[b], in_=w_ada[l, :, v * D:(v + 1) * D].rearrange("(k p) n -> p k n", p=128)),
                   writes=[wname], dma_key=wname)
                pbank = pp[ji % 2]
                pname = PPN[ji % 2]
                pv = pbank[:, 0:8 * (NS + 1)].rearrange("p (m t) -> p m t", m=8)
                for m in range(8):
                    for k in range(8):
                        op("pe", lambda e, b=b, m=m, k=k, pv=pv: e.matmul(
                            pv[:, m, :], lhsT=wa[b][:, k, m * 128:(m + 1) * 128], rhs=cT[:, k, :],
                            start=(k == 0), stop=(k == 7)),
                           reads=[wname, "cT"], writes=[pname])
                op("dve", lambda e, l=l, v=v, pv=pv: e.tensor_tensor(
                    out=modT[:, l, v, :, :], in0=pv,
                    in1=baT_t[:, l, v, :].unsqueeze(2).to_broadcast([128, 8, NS + 1]), op=ALU.add),
                   reads=[pname, "baT_t"], writes=["modT"])
                if v in (1, 4):
                    gi = l if v == 1 else 2 + l
                    op("dve", lambda e, l=l, v=v, gi=gi: e.scalar_tensor_tensor(
                        out=modT[:, l, v, :, :], in0=modT[:, l, v, :, :], scalar=1.0,
                        in1=gT_t[:, gi, :].unsqueeze(2).to_broadcast([128, 8, NS + 1]),
                        op0=ALU.add, op1=ALU.mult),
                       reads=["modT", "gT_t"], writes=["modT"])
                ji += 1

        import os
        STOP = int(os.environ.get("MK_STOP", "-1"))

        class _Stop(Exception):
            pass

        def chk(n):
            if STOP == n:
                raise _Stop()

        def build_gt(l, v, sample):
            if not sample:
                op("dve", lambda e: e.tensor_copy(
                    out=tmp[:].rearrange("p (k t) -> p k t", k=8),
                    in_=modT[:, l, v, :, 0:1].to_broadcast([128, 8, 128])),
                   reads=["modT"], writes=["tmp"])
                for k in range(8):
                    bk = pTf[k // 4]
                    op("pe", lambda e, k=k, bk=bk: e.transpose(
                        out=bk[:, (k % 4) * 128:(k % 4 + 1) * 128], in_=tmp[:, k * 128:(k + 1) * 128], identity=identf[:]),
                       reads=["tmp", "identf"], writes=["pT" if k < 4 else "pT2"])
                op("act", lambda e: e.copy(out=GTp[:, 0:512], in_=pTf[0][:]), reads=["pT"], writes=["GTp"])
                op("act", lambda e: e.copy(out=GTp[:, 512:1024], in_=pTf[1][:]), reads=["pT2"], writes=["GTp"])
            else:
                for k in range(8):
                    bk = pTf[k // 4]
                    op("pe", lambda e, k=k, bk=bk: e.transpose(
                        out=bk[0:NS, (k % 4) * 128:(k % 4 + 1) * 128], in_=modT[:, l, v, k, 1:NS + 1], identity=identf[:]),
                       reads=["modT", "identf"], writes=["pT" if k < 4 else "pT2"])
                op("act", lambda e: e.copy(out=GTp[0:NS, 0:512], in_=pTf[0][0:NS, :]), reads=["pT"], writes=["GTp"])
                op("act", lambda e: e.copy(out=GTp[0:NS, 512:1024], in_=pTf[1][0:NS, :]), reads=["pT2"], writes=["GTp"])

        def norm_mod(xc, xname, Pn, l, vsh, vsc, sample, hdst=None, hname="hT0", sc=0):
            if hdst is None:
                hdst = hT
            op("act", lambda e: e.activation(out=xn[0:Pn, :], in_=xc, func=AF.Square, scale=1.0 / 32.0,
                                             accum_out=stt[0:Pn, sc:sc + 1]),
               reads=[xname], writes=["xn", "sttn%d" % sc])
            op("dve", lambda e: e.tensor_scalar(out=stt[0:Pn, sc:sc + 1], in0=stt[0:Pn, sc:sc + 1], scalar1=EPS, scalar2=None,
                                                op0=ALU.add), reads=["sttn%d" % sc], writes=["sttn%d" % sc])
            op("pool", lambda e: e.tensor_tensor(out=stt[0:Pn, sc + 1:sc + 2], in0=stt[0:Pn, sc:sc + 1], in1=mh[0:Pn, 0:1], op=ALU.pow),
               reads=["sttn%d" % sc, "mh"], writes=["sttn%d" % sc])
            op("act", lambda e: e.activation(out=xn[0:Pn, :], in_=xc, func=AF.Copy, scale=stt[0:Pn, sc + 1:sc + 2]),
               reads=[xname, "sttn%d" % sc], writes=["xn"])
            for k in range(8):
                op("pe", lambda e, k=k: e.transpose(out=pT[:, k, 0:Pn], in_=xn[0:Pn, k * 128:(k + 1) * 128],
                                                    identity=identb[0:Pn, 0:Pn]),
                   reads=["xn", "identb"], writes=["pT"])
            if sample:
                gtab = modT[:, l, vsc, :, 1:NS + 1]
                stab = modT[:, l, vsh, :, 1:NS + 1]
                t3 = tmp[:].rearrange("p (k t) -> p k t", k=8)[:, :, 0:Pn]
                op("dve", lambda e: e.tensor_tensor(out=t3, in0=pT[:, :, 0:Pn], in1=gtab, op=ALU.mult),
                   reads=["pT", "modT"], writes=["tmp"])
                op("pool", lambda e: e.tensor_tensor(out=hdst[:, :, 0:Pn], in0=t3, in1=stab, op=ALU.add),
                   reads=["tmp", "modT"], writes=[hname])
            else:
                for k in range(8):
                    op("act", lambda e, k=k: e.activation(out=hdst[:, k, 0:Pn], in_=pT[:, k, 0:Pn], func=AF.Identity,
                                                          scale=modT[:, l, vsc, k, 0:1], bias=modT[:, l, vsh, k, 0:1]),
                       reads=["pT", "modT"], writes=[hname])

        def group_norm(src, srcname, Pn, gain, gname, out, oname, cbuf, cnames, sbuf_, snames, col0, stn):
            m = stt[0:Pn, col0:col0 + 4]
            vv_ = stt[0:Pn, col0 + 4:col0 + 8]
            r = stt[0:Pn, col0 + 8:col0 + 12]
            cen3 = cbuf[0:Pn, :].rearrange("p (g c) -> p g c", g=4)
            sq3 = sbuf_[0:Pn, :].rearrange("p (g c) -> p g c", g=4)
            op("dve", lambda e: e.tensor_reduce(out=m, in_=src, axis=AX.X, op=ALU.add), reads=[srcname], writes=[stn])
            op("dve", lambda e: e.tensor_scalar(out=m, in0=m, scalar1=1.0 / 128.0, scalar2=None, op0=ALU.mult),
               reads=[stn], writes=[stn])
            op("dve", lambda e: e.tensor_tensor(out=cen3, in0=src, in1=m.unsqueeze(2).to_broadcast([Pn, 4, 128]),
                                                op=ALU.subtract), reads=[srcname, stn], writes=cnames)
            op("pool", lambda e: e.tensor_tensor(out=sq3, in0=cen3, in1=cen3, op=ALU.mult),
               reads=cnames, writes=snames)
            op("dve", lambda e: e.tensor_reduce(out=vv_, in_=sq3, axis=AX.X, op=ALU.add),
               reads=snames, writes=[stn])
            op("dve", lambda e: e.tensor_scalar(out=vv_, in0=vv_, scalar1=1.0 / 128.0, scalar2=EPS, op0=ALU.mult,
                                                op1=ALU.add), reads=[stn], writes=[stn])
            op("pool", lambda e: e.tensor_tensor(out=r, in0=vv_, in1=mh[0:Pn, :], op=ALU.pow),
               reads=[stn, "mh"], writes=[stn])
            op("dve", lambda e: e.tensor_tensor(out=cen3, in0=cen3, in1=r.unsqueeze(2).to_broadcast([Pn, 4, 128]),
                                                op=ALU.mult), reads=cnames + [stn], writes=cnames)
            op("pool", lambda e: e.tensor_tensor(out=out, in0=cbuf[0:Pn, :], in1=gain[0:Pn, :], op=ALU.mult),
               reads=cnames + [gname], writes=[oname])

        def gelu(src, srcname, Pn, out, oname):
            op("act", lambda e: e.activation(out=hx[0:Pn, :], in_=src, func=AF.Copy, scale=0.5),
               reads=[srcname], writes=["hx"])
            op("act", lambda e: e.activation(out=gA[0:Pn, :], in_=src, func=AF.Square), reads=[srcname], writes=["gA"])
            op("dve", lambda e: e.tensor_scalar(out=gA[0:Pn, :], in0=gA[0:Pn, :], scalar1=0.044715, scalar2=1.0,
                                                op0=ALU.mult, op1=ALU.add), reads=["gA"], writes=["gA"])
            op("dve", lambda e: e.tensor_tensor(out=gA[0:Pn, :], in0=gA[0:Pn, :], in1=hx[0:Pn, :], op=ALU.mult),
               reads=["gA", "hx"], writes=["gA"])
            op("act", lambda e: e.activation(out=gA[0:Pn, :], in_=gA[0:Pn, :], func=AF.Tanh,
                                             scale=2.0 * math.sqrt(2.0 / math.pi)), reads=["gA"], writes=["gA"])
            op("dve", lambda e: e.scalar_tensor_tensor(out=out, in0=gA[0:Pn, :], scalar=1.0, in1=hx[0:Pn, :],
                                                       op0=ALU.add, op1=ALU.mult), reads=["gA", "hx"], writes=[oname])

        def rope(src, srcname, Pn, tab, tabname, coff, out, oname):
            s3 = src.rearrange("p (h d) -> p h d", h=4)
            x1 = s3[:, :, 0:64]
            x2 = s3[:, :, 64:128]
            cosb = tab[:, coff:coff + 64].unsqueeze(1).to_broadcast([Pn, 4, 64])
            sinb = tab[:, coff + 64:coff + 128].unsqueeze(1).to_broadcast([Pn, 4, 64])
            a = t1[0:Pn, :].rearrange("p (h d) -> p h d", h=4)
            b_ = t2[0:Pn, :].rearrange("p (h d) -> p h d", h=4)
            o3 = out.rearrange("p (h d) -> p h d", h=4)
            op("dve", lambda e: e.tensor_tensor(out=a, in0=x1, in1=cosb, op=ALU.mult), reads=[srcname, tabname], writes=["t1"])
            op("dve", lambda e: e.tensor_tensor(out=b_, in0=x2, in1=sinb, op=ALU.mult), reads=[srcname, tabname], writes=["t2"])
            op("pool", lambda e: e.tensor_tensor(out=o3[:, :, 0:64], in0=a, in1=b_, op=ALU.subtract),
               reads=["t1", "t2"], writes=[oname])
            op("dve", lambda e: e.tensor_tensor(out=a, in0=x1, in1=sinb, op=ALU.mult), reads=[srcname, tabname, oname], writes=["t1"])
            op("dve", lambda e: e.tensor_tensor(out=b_, in0=x2, in1=cosb, op=ALU.mult), reads=[srcname, tabname, oname], writes=["t2"])
            op("pool", lambda e: e.tensor_tensor(out=o3[:, :, 64:128], in0=a, in1=b_, op=ALU.add),
               reads=["t1", "t2"], writes=[oname])

        def mix_ctx(c, sample):
            par = 0 if sample else c % 2
            Pn = NS if sample else 128
            if sample:
                return par, Pn, xs_t[0:NS, :], "xs_t", ropes_t[0:NS, :], "ropes_t"
            return par, Pn, xres[:, c, :], "x%d" % c, ropeb[:, c % 2, :], "ropeb%d" % (c % 2)

        def mix_front1(l, c, sample):
            par, Pn, xc, xname, rtab, rname = mix_ctx(c, sample)
            if not sample:
                op("sp", lambda e: e.dma_start(out=rtab, in_=ropep[c]),
                   writes=[rname] + (PH2_NAMES if c < 2 else []), dma_key=rname)
            norm_mod(xc, xname, Pn, l, 0, 1, sample, hdst=hT2[:, :, par * 128:(par + 1) * 128], hname="hT%d" % par, sc=2 * par)
            for n in range(3):
                for k in range(8):
                    op("pe", lambda e, n=n, k=k: e.matmul(pp[n][0:Pn, :], lhsT=hT2[:, k, par * 128:par * 128 + Pn],
                                                           rhs=Win[:, k, n * 512:(n + 1) * 512],
                                                           start=(k == 0), stop=(k == 7)),
                       reads=["hT%d" % par, "w_in"], writes=[PPN[n]])

        def mix_front2(l, c, sample):
            par, Pn, xc, xname, rtab, rname = mix_ctx(c, sample)
            rope(pp[0][0:Pn, :], PPN[0], Pn, rtab, rname, 0, qr[par][0:Pn, :], "qr%d" % par)
            rope(pp[1][0:Pn, :], PPN[1], Pn, rtab, rname, 128, kr[par][0:Pn, :], "kr%d" % par)
            op("act", lambda e: e.copy(out=vb[par][0:Pn, :], in_=pp[2][0:Pn, :]), reads=[PPN[2]], writes=["vb%d" % par])
            for n in range(3):
                for k in range(8):
                    op("pe", lambda e, n=n, k=k: e.matmul(pp[n][0:Pn, :], lhsT=hT2[:, k, par * 128:par * 128 + Pn],
                                                           rhs=Win[:, k, (3 + n) * 512:(4 + n) * 512],
                                                           start=(k == 0), stop=(k == 7)),
                       reads=["hT%d" % par, "w_in"], writes=[PPN[n]])
            op("act", lambda e: e.activation(out=sg[par][0:Pn, :], in_=pp[0][0:Pn, :], func=AF.Silu),
               reads=[PPN[0]], writes=["sg%d" % par])
            gelu(pp[1][0:Pn, :], PPN[1], Pn, gu[par][0:Pn, :], "gu%d" % par)
            gelu(pp[2][0:Pn, :], PPN[2], Pn, gv[0:Pn, :], "gv")
            gv3 = gv[0:Pn, :].rearrange("p (g c) -> p g c", g=4)
            if sample:
                group_norm(gv3, "gv", Pn, ln_tab, "ln_tab", vnf[0:Pn, :], "sg1", hx, ["hx"], gA, ["gA"], 16, "sttf")
                op("act", lambda e: e.copy(out=vn[0][0:Pn, :], in_=vnf[0:Pn, :]), reads=["sg1"], writes=["vn0"])
                op("sp", lambda e: e.dma_start(out=nvs[l], in_=vnf[0:NS, :]), reads=["sg1"], dma_key="o_nvs")
            else:
                group_norm(gv3, "gv", Pn, ln_tab, "ln_tab", vn[par][0:Pn, :], "vn%d" % par, hx, ["hx"], gA, ["gA"], 16, "sttf")

        def mix_back(l, c, sample, hook=None):
            par, Pn, xc, xname, rtab, rname = mix_ctx(c, sample)
            qrn, krn, vbn, sgn, gun, vnn = ["%s%d" % (b_, par) for b_ in ("qr", "kr", "vb", "sg", "gu", "vn")]
            qr_, kr_, vb_, sg_, gu_, vn_ = qr[par], kr[par], vb[par], sg[par], gu[par], vn[par]
            if not sample:
                op("dve", lambda e: e.tensor_tensor(out=kk[:].rearrange("p (h d) -> p h d", h=4),
                                                    in0=kr_[:].rearrange("p (h d) -> p h d", h=4),
                                                    in1=kdec[:].unsqueeze(2).to_broadcast([128, 4, 128]), op=ALU.mult),
                   reads=[krn, "kdec"], writes=["kk"])
                for h in range(4):
                    op("pe", lambda e, h=h: e.transpose(out=pT2[:, h, :], in_=qr_[:, h * 128:(h + 1) * 128], identity=identb[:]),
                       reads=[qrn, "identb"], writes=["pT2"])
                for h in range(4):
                    op("pe", lambda e, h=h: e.transpose(out=pT2[:, 4 + h, :], in_=kr_[:, h * 128:(h + 1) * 128], identity=identb[:]),
                       reads=[krn, "identb"], writes=["pT2"])
                op("act", lambda e: e.copy(out=qkT[:], in_=pT2[:]), reads=["pT2"], writes=["qkT"])
                op("dve", lambda e: e.tensor_tensor(out=q2T, in0=pT2[:, 0:4, :], in1=qdecT, op=ALU.mult),
                   reads=["pT2", "qdecT"], writes=["q2T"])
                pa = pp[3][:].rearrange("p (h t) -> p h t", h=4)
                for h in range(4):
                    op("pe", lambda e, h=h: e.matmul(pa[:, h, :], lhsT=qkT[:, 4 + h, :], rhs=qkT[:, h, :], start=True, stop=True),
                       reads=["qkT"], writes=[PPN[3]])
                op("dve", lambda e: e.tensor_tensor(out=attTm[:], in0=pa, in1=dmaskT, op=ALU.mult),
                   reads=[PPN[3], "dmaskT"], writes=["attTm"])
                po = pp[4][:].rearrange("p (h t) -> p h t", h=4)
                for h in range(4):
                    op("pe", lambda e, h=h: e.matmul(po[:, h, :], lhsT=attTm[:, h, :], rhs=vb_[:, h * 128:(h + 1) * 128],
                                                      start=True, stop=False), reads=["attTm", vbn], writes=[PPN[4]])
                    op("pe", lambda e, h=h: e.matmul(po[:, h, :], lhsT=q2T[:, h, :], rhs=S_b[:, h, :],
                                                      start=False, stop=True), reads=["q2T", "S_b"], writes=[PPN[4]])
                pS = pp[5][:].rearrange("p (h t) -> p h t", h=4)
                for h in range(4):
                    op("pe", lambda e, h=h: e.matmul(pS[:, h, :], lhsT=kk[:, h * 128:(h + 1) * 128], rhs=vb_[:, h * 128:(h + 1) * 128],
                                                      start=True, stop=True), reads=["kk", vbn], writes=[PPN[5]])
                for h in range(4):
                    op("dve", lambda e, h=h: e.scalar_tensor_tensor(out=S_f[:, h, :], in0=S_f[:, h, :], scalar=GAM[h] ** 128,
                                                                    in1=pS[:, h, :], op0=ALU.mult, op1=ALU.add),
                       reads=["S_f", PPN[5]], writes=["S_f"])
                op("act", lambda e: e.copy(out=S_b, in_=S_f), reads=["S_f"], writes=["S_b"])
                osrc, osname = po, PPN[4]
            else:
                pr3 = t12[0:NS, :]
                op("dve", lambda e: e.tensor_tensor(out=pr3, in0=qr_[0:NS, :], in1=kr_[0:NS, :], op=ALU.mult),
                   reads=[qrn, krn], writes=["t1", "t2"])
                op("dve", lambda e: e.tensor_reduce(out=stt[0:NS, 28:32], in_=pr3.rearrange("p (h d) -> p h d", h=4),
                                                    axis=AX.X, op=ALU.add), reads=["t1", "t2"], writes=["sttq"])
                op("dve", lambda e: e.tensor_tensor(out=o1[0:NS, :].rearrange("p (h d) -> p h d", h=4),
                                                    in0=vb_[0:NS, :].rearrange("p (h d) -> p h d", h=4),
                                                    in1=stt[0:NS, 28:32].unsqueeze(2).to_broadcast([NS, 4, 128]), op=ALU.mult),
                   reads=[vbn, "sttq"], writes=["cen"])
                for h in range(4):
                    op("pe", lambda e, h=h: e.transpose(out=pT2[:, h, 0:NS], in_=qr_[0:NS, h * 128:(h + 1) * 128],
                                                        identity=identb[0:NS, 0:NS]), reads=[qrn, "identb"], writes=["pT2"])
                op("act", lambda e: e.copy(out=qkT[:, 0:4, 0:NS], in_=pT2[:, 0:4, 0:NS]), reads=["pT2"], writes=["qkT"])
                op("dve", lambda e: e.tensor_tensor(
                    out=Qm[:], in0=qkT[:, 0:4, 0:NS].unsqueeze(3).to_broadcast([128, 4, NS, NS]),
                    in1=delta16.unsqueeze(1).to_broadcast([128, 4, NS, NS]), op=ALU.mult),
                   reads=["qkT", "delta16"], writes=["mixin"])
                for g2 in range(NS // 2):
                    op("sp", lambda e, g2=g2: e.dma_start(
                        out=Sg_f[:], in_=sret[l, g2 * 2:(g2 + 1) * 2].rearrange("b h d e -> d b h e")),
                       writes=["hx", "gA"], dma_key="sgf")
                    op("pool", lambda e, g2=g2: e.dma_start(
                        out=Sg_b[:], in_=sret[l, g2 * 2:(g2 + 1) * 2].rearrange("b h d e -> d b h e")),
                       writes=["gu1"], dma_key="sgb")
                    op("dve", lambda e, g2=g2: e.tensor_tensor(
                        out=Km[0:NS, :, :], in0=kr_[0:NS, :].unsqueeze(1).to_broadcast([NS, 2, 512]),
                        in1=deltaK[0:NS, g2 * 2:(g2 + 1) * 2].unsqueeze(2).to_broadcast([NS, 2, 512]), op=ALU.mult),
                       reads=[krn, "deltaK"], writes=["qr1", "kr1"])
                    for bl in range(2):
                        bq = g2 * 2 + bl
                        for h in range(4):
                            op("pe", lambda e, bl=bl, bq=bq, h=h: e.matmul(
                                pp[h][0:NS, 0:128], lhsT=Qm[:, h, bq, :], rhs=Sg_b[:, bl, h, :],
                                start=(bq == 0), stop=(bq == NS - 1)),
                               reads=["mixin", "gu1"], writes=[PPN[h]])
                    pSg = [pp[4][:].rearrange("p (h t) -> p h t", h=4), pp[5][:].rearrange("p (h t) -> p h t", h=4)]
                    for bl in range(2):
                        for h in range(4):
                            op("pe", lambda e, bl=bl, h=h: e.matmul(
                                pSg[bl][:, h, :], lhsT=Km[0:NS, bl, h * 128:(h + 1) * 128], rhs=vb_[0:NS, h * 128:(h + 1) * 128],
                                start=True, stop=True), reads=["qr1", "kr1", vbn], writes=[PPN[4 + bl]])
                    for bl in range(2):
                        for h in range(4):
                            op("dve", lambda e, bl=bl, h=h: e.scalar_tensor_tensor(
                                out=Sg_f[:, bl, h, :], in0=Sg_f[:, bl, h, :], scalar=GAM[h], in1=pSg[bl][:, h, :],
                                op0=ALU.mult, op1=ALU.add), reads=["hx", "gA", PPN[4 + bl]], writes=["hx", "gA"])
                    op("sp", lambda e, g2=g2: e.dma_start(
                        out=nrs[l, g2 * 2:(g2 + 1) * 2].rearrange("b h d e -> d b h e"), in_=Sg_f[:]),
                       reads=["hx", "gA"], dma_key="o_nrs")
                for h in range(4):
                    op("dve", lambda e, h=h: e.scalar_tensor_tensor(
                        out=o_s[0:NS, h * 128:(h + 1) * 128], in0=pp[h][0:NS, 0:128], scalar=GAM[h],
                        in1=o1[0:NS, h * 128:(h + 1) * 128], op0=ALU.mult, op1=ALU.add),
                       reads=[PPN[h], "cen"], writes=["gv"])
                osrc, osname = o_s[0:NS, :].rearrange("p (h d) -> p h d", h=4), "gv"
            if hook is not None:
                hook()
            ps_s = pp[3]
            for g in range(4):
                if sample:
                    op("pe", lambda e, g=g: e.matmul(ps_s[0:NS, g * 128:(g + 1) * 128], lhsT=wsS_b[0:NS, g, :],
                                                      rhs=vn_[0:NS, g * 128:(g + 1) * 128], start=True, stop=True),
                       reads=["wsS_b", vnn], writes=[PPN[3]])
                else:
                    op("pe", lambda e, g=g: e.matmul(ps_s[:, g * 128:(g + 1) * 128], lhsT=wsT_b[:, g, :],
                                                      rhs=vn_[:, g * 128:(g + 1) * 128], start=True, stop=True),
                       reads=["wsT_b", vnn], writes=[PPN[3]])
            for g in range(4):
                bcol = b00[0:NS, g:g + 1] if sample else bsT_t[:, l, g:g + 1]
                op("dve", lambda e, g=g, bcol=bcol: e.scalar_tensor_tensor(
                    out=mixin[0:Pn, 512 + g * 128:512 + (g + 1) * 128], in0=ps_s[0:Pn, g * 128:(g + 1) * 128],
                    scalar=bcol, in1=gu_[0:Pn, g * 128:(g + 1) * 128], op0=ALU.add, op1=ALU.mult),
                   reads=[PPN[3], gun, "b00", "bsT_t"], writes=["mixin"])
            group_norm(osrc, osname, Pn, gn_tab, "gn_tab", cen[0:Pn, :], "cen", cen, ["cen"], t12, ["t1", "t2"], 4, "sttb")
            op("dve", lambda e: e.tensor_tensor(out=mixin[0:Pn, 0:512], in0=cen[0:Pn, :], in1=sg_[0:Pn, :], op=ALU.mult),
               reads=["cen", sgn], writes=["mixin"])
            for k in range(8):
                op("pe", lambda e, k=k: e.transpose(out=pT[:, k, 0:Pn], in_=mixin[0:Pn, k * 128:(k + 1) * 128],
                                                    identity=identb[0:Pn, 0:Pn]), reads=["mixin", "identb"], writes=["pT"])
            op("act", lambda e: e.copy(out=mixinT[:, :, 0:Pn], in_=pT[:, :, 0:Pn]), reads=["pT"], writes=["qkT"])
            obanks, onames = [pp[3], pp[5]], [PPN[3], PPN[5]]
            for n in range(2):
                for k in range(8):
                    op("pe", lambda e, n=n, k=k: e.matmul(obanks[n][0:Pn, :], lhsT=mixinT[:, k, 0:Pn],
                                                           rhs=Wout[:, k, n * 512:(n + 1) * 512], start=(k == 0), stop=(k == 7)),
                       reads=["qkT", "w_out"], writes=[onames[n]])
            resid_update(xc, xname, Pn, obanks, onames, sample)

        def resid_update(xc, xname, Pn, banks2, names2, sample):
            gt = GTp
            gname = "GTp"
            for n in range(2):
                op("dve", lambda e, n=n: e.tensor_tensor(out=tmp[0:Pn, n * 512:(n + 1) * 512], in0=banks2[n][0:Pn, :],
                                                         in1=gt[0:Pn, n * 512:(n + 1) * 512], op=ALU.mult),
                   reads=[names2[n], gname], writes=["tmp"])
            op("pool", lambda e: e.tensor_tensor(out=xc, in0=xc, in1=tmp[0:Pn, :], op=ALU.add),
               reads=[xname, "tmp"], writes=[xname])

        def ffn_front(l, t):
            for cc in range(2):
                c = 2 * t + cc
                norm_mod(xres[:, c, :], "x%d" % c, 128, l, 3, 4, False, hdst=hT2[:, :, cc * 128:(cc + 1) * 128], hname="hT%d" % cc, sc=2 * cc)

        def ffn_tile(l, t, sample, mid=None):
            N = NS if sample else 256
            if sample:
                norm_mod(xs_t[0:NS, :], "xs_t", NS, l, 3, 4, True)
            for j in range(16):
                par = 0 if sample else j % 2
                for s_, m in enumerate((j, j + 16)):
                    pb = pp[(2 * j + s_) % 4]
                    pbn = PPN[(2 * j + s_) % 4]
                    for k in range(8):
                        op("pe", lambda e, m=m, k=k, pb=pb: e.matmul(pb[:, 0:N], lhsT=Wup[:, k, m * 128:(m + 1) * 128],
                                                                      rhs=hT2[:, k, 0:N], start=(k == 0), stop=(k == 7)),
                           reads=["w_up", "hT0", "hT1"], writes=[pbn])
                    un = "U%d%d" % (par, s_)
                    tn = "tb%d%d" % (par, s_)
                    tbv = tb[:, par, s_, 0:N]
                    op("act", lambda e, m=m, pb=pb, tbv=tbv: e.activation(
                        out=tbv, in_=pb[:, 0:N], func=AF.Identity, scale=cw[:, l, m, 2:3], bias=cb[:, l, m:m + 1]),
                       reads=[pbn, "cw", "cb"], writes=[tn])
                    if sample:
                        op("act", lambda e, m=m, pb=pb: e.copy(out=upTs[:, m, :], in_=pb[:, 0:NS]),
                           reads=[pbn], writes=["tmp"])
                        x1, x0 = scT1[:, m, :], scT0[:, m, :]
                        rd = ["U10", "U11", "tb10", "tb11"]
                    else:
                        Uv = U[:, par, s_, :]
                        op("pool", lambda e, m=m, Uv=Uv: e.tensor_copy(out=Uv[:, 0:2], in_=hist[:, m, :]),
                           reads=["hist"], writes=[un])
                        op("act", lambda e, pb=pb, Uv=Uv: e.copy(out=Uv[:, 2:258], in_=pb[:, 0:256]),
                           reads=[pbn], writes=[un])
                        op("pool", lambda e, m=m, Uv=Uv: e.tensor_copy(out=hist[:, m, :], in_=Uv[:, 256:258]),
                           reads=[un], writes=["hist"])
                        x1, x0 = Uv[:, 1:257], Uv[:, 0:256]
                        rd = [un]
                    op("dve", lambda e, m=m, x1=x1, tbv=tbv: e.scalar_tensor_tensor(
                        out=tbv, in0=x1, scalar=cw[:, l, m, 1:2], in1=tbv, op0=ALU.mult, op1=ALU.add),
                       reads=rd + ["cw", tn], writes=[tn])
                    op("dve", lambda e, m=m, x0=x0, tbv=tbv: e.scalar_tensor_tensor(
                        out=tbv, in0=x0, scalar=cw[:, l, m, 0:1], in1=tbv, op0=ALU.mult, op1=ALU.add),
                       reads=rd + ["cw", tn], writes=[tn])
                sn = "sl%d" % par
                op("act", lambda e, par=par: e.activation(out=sl[:, par, 0:N], in_=tb[:, par, 0, 0:N], func=AF.Silu),
                   reads=["tb%d0" % par], writes=[sn])
                op("dve", lambda e, j=j, par=par: e.tensor_tensor(out=fT[:, j, 0:N], in0=sl[:, par, 0:N], in1=tb[:, par, 1, 0:N],
                                                                  op=ALU.mult), reads=[sn, "tb%d1" % par], writes=["fT"])
            if mid is not None:
                mid()
            if sample:
                op("sp", lambda e: e.dma_start(out=ncsT[:, l, 1, :, :], in_=upTs), reads=["tmp"], dma_key="o_ncs")
                banks_seq = [([pp[4], pp[5]], [PPN[4], PPN[5]])]
            else:
                banks_seq = [([pp[4], pp[5]], [PPN[4], PPN[5]]), ([bank[1], pp[4]], ["pT2", PPN[4]])]
            for cc, (bks, bnames) in enumerate(banks_seq):
                Pn = NS if sample else 128
                for n in range(2):
                    for j in range(16):
                        op("pe", lambda e, n=n, j=j, cc=cc, bks=bks, Pn=Pn: e.matmul(
                            bks[n][0:Pn, :], lhsT=fT[:, j, cc * 128:cc * 128 + Pn],
                            rhs=Wdn[:, j, n * 512:(n + 1) * 512], start=(j == 0), stop=(j == 15)),
                           reads=["fT", "w_dn"], writes=[bnames[n]])
                if sample:
                    resid_update(xs_t[0:NS, :], "xs_t", NS, bks, bnames, True)
                else:
                    c = 2 * t + cc
                    resid_update(xres[:, c, :], "x%d" % c, 128, bks, bnames, False)

        def load_w(dst, src3, name, extra, nsplit):
            K = dst.shape[1]
            N = dst.shape[2]
            step = N // nsplit
            for i in range(nsplit):
                op("pool", lambda e, i=i: e.dma_start(
                    out=dst[:, :, i * step:(i + 1) * step],
                    in_=src3[:, i * step:(i + 1) * step].rearrange("(k p) n -> p k n", p=128)),
                   writes=[name] + extra, dma_key=name)

        try:
          chk(0)
          for l in range(2):
                op("dve", lambda e: e.memset(stt[:, 32:33], 0.0), writes=["w_dn"] + AU_NAMES + ["fence"])
                load_w(Win, w_in[l], "w_in", ["wa0", "wa1", "w_up"], 3)
                load_w(Wout, w_out[l], "w_out", ["wa0", "wa1", "w_up"], 1)

                def ld1(dst, src, name):
                    op("sp", lambda e: e.dma_start(out=dst, in_=src), writes=[name] + PH2_NAMES, dma_key="lp")

                ld1(gn_tab[:], gng[l:l + 1, :].partition_broadcast(128), "gn_tab")
                ld1(ln_tab[:], lng[l:l + 1, :].partition_broadcast(128), "ln_tab")
                ld1(ropes_t[0:NS, :], ropes, "ropes_t")
                ld1(dmaskT, dmaskT_d, "dmaskT")
                ld1(qdecT, qdecT_d, "qdecT")
                ld1(trilT, trilT_d, "trilT")
                ld1(delta16, delta16_d, "delta16")
                op("sp", lambda e, l=l: e.dma_start(out=tmp[:, 0:512].rearrange("p (g t) -> p g t", g=4), in_=wsT[l]),
                   writes=["tmp"], dma_key="lp")
                op("sp", lambda e, l=l: e.dma_start(out=w00[:], in_=ws00[l:l + 1, :].partition_broadcast(128)),
                   writes=["w00"], dma_key="lp")
                op("sp", lambda e, l=l: e.dma_start(out=b00[:], in_=bs0[l:l + 1, :].partition_broadcast(128)),
                   writes=["b00"], dma_key="lp")
                op("dve", lambda e: e.tensor_tensor(out=wsT_b, in0=tmp[:, 0:512].rearrange("p (g t) -> p g t", g=4),
                                                    in1=trilT.unsqueeze(1).to_broadcast([128, 4, 128]),
                                                    op=ALU.mult), reads=["tmp", "trilT"], writes=["wsT_b"] + PH2_NAMES)
                for g in range(4):
                    op("dve", lambda e, g=g: e.tensor_scalar(out=wsS_b[0:NS, g, :], in0=identf[0:NS, 0:NS], scalar1=w00[0:NS, g:g + 1],
                                                             scalar2=None, op0=ALU.mult), reads=["identf", "w00"], writes=["wsS_b"])
                chk(10 * l + 1)
                build_gt(l, 2, False)
                op("dve", lambda e: e.memset(S_f, 0.0), writes=["S_f"] + PH2_NAMES)
                op("dve", lambda e: e.memset(S_b, 0.0), writes=["S_b"] + PH2_NAMES)
                chk(10 * l + 2)
                mix_front1(l, 0, False)
                mix_front2(l, 0, False)
                for c in range(NCH):
                    nxt = (lambda l=l, c=c: mix_front1(l, c + 1, False)) if c + 1 < NCH else None
                    mix_back(l, c, False, hook=nxt)
                    if c + 1 < NCH:
                        mix_front2(l, c + 1, False)
                    chk(10 * l + 3)
                op("sp", lambda e, l=l: e.dma_start(out=nrp[l], in_=S_f), reads=["S_f"], dma_key="o_nrp")
                chk(10 * l + 4)
                build_gt(l, 2, True)
                mix_front1(l, 0, True)
                mix_front2(l, 0, True)
                mix_back(l, 0, True)
                chk(10 * l + 5)
                load_w(Wup, w_up[l], "w_up", ["w_in", "w_out"], 4)
                load_w(Wdn, w_down[l], "w_dn", AU_NAMES, 2)
                op("dve", lambda e: e.memset(stt[:, 33:34], 0.0), writes=PH1_NAMES + PH2_NAMES + ["fence2"])
                build_gt(l, 5, False)
                op("pool", lambda e: e.memset(hist[:], 0.0), writes=["hist"])
                chk(10 * l + 6)
                ffn_front(l, 0)
                for t in range(NCH // 2):
                    ffn_tile(l, t, False, mid=(lambda l=l, t=t: ffn_front(l, t + 1)) if t + 1 < NCH // 2 else None)
                    chk(10 * l + 7)
                op("sp", lambda e, l=l: e.dma_start(out=ncpT[:, l, :, :], in_=hist[:]), reads=["hist"], dma_key="o_ncp")
                chk(10 * l + 8)
                op("sp", lambda e, l=l: e.dma_start(out=scT0, in_=sconvT[:, l, 0]), writes=["U10", "U11"], dma_key="lp")
                op("sp", lambda e, l=l: e.dma_start(out=scT1, in_=sconvT[:, l, 1]), writes=["tb10", "tb11"], dma_key="lp")
                build_gt(l, 5, True)
                ffn_tile(l, 0, True)
                op("sp", lambda e, l=l: e.dma_start(out=ncsT[:, l, 0, :, :], in_=sconvT[:, l, 1, :, :]), dma_key="o_ncs0")

        except _Stop:
            P.emit()
            return nc

        op("sp", lambda e: e.dma_start(out=GTp[:], in_=gfin.partition_broadcast(128)), writes=["GTp"], dma_key="lp")

        def final_norm(xc, xname, Pn, dst, slot):
            slot = 0
            yo = tmp
            yn = "tmp"
            op("act", lambda e: e.activation(out=yo[0:Pn, :], in_=xc, func=AF.Square, scale=1.0 / 32.0,
                                             accum_out=stt[0:Pn, 40 + slot:41 + slot]),
               reads=[xname], writes=[yn, "stt%d" % slot])
            op("dve", lambda e: e.tensor_scalar(out=stt[0:Pn, 40 + slot:41 + slot], in0=stt[0:Pn, 40 + slot:41 + slot],
                                                scalar1=EPS, scalar2=None, op0=ALU.add),
               reads=["stt%d" % slot], writes=["stt%d" % slot])
            op("pool", lambda e: e.tensor_tensor(out=stt[0:Pn, 44 + slot:45 + slot], in0=stt[0:Pn, 40 + slot:41 + slot],
                                                 in1=mh[0:Pn, 0:1], op=ALU.pow), reads=["stt%d" % slot, "mh"], writes=["stt%d" % slot])
            op("act", lambda e: e.activation(out=yo[0:Pn, :], in_=xc, func=AF.Copy, scale=stt[0:Pn, 44 + slot:45 + slot]),
               reads=[xname, "stt%d" % slot], writes=[yn])
            op("dve", lambda e: e.tensor_tensor(out=yo[0:Pn, :], in0=yo[0:Pn, :], in1=GTp[0:Pn, :], op=ALU.mult),
               reads=[yn, "GTp"], writes=[yn])
            op("sp", lambda e: e.dma_start(out=dst, in_=yo[0:Pn, :]), reads=[yn], dma_key="o_y%d" % slot)

        for c in range(NCH):
            final_norm(xres[:, c, :], "x%d" % c, 128, yp[c * 128:(c + 1) * 128, :], c % 2)
        final_norm(xs_t[0:NS, :], "xs_t", NS, ys, 0)

        P.emit()
    return nc


def _consts():
    half = 64
    freqs = np.exp(-math.log(10000.0) * np.arange(half, dtype=np.float32) / half).astype(np.float32)
    sc = np.float32(128.0 ** -0.5)

    def tab(pos):
        ang = pos.astype(np.float32)[:, None] * freqs[None, :]
        c = np.cos(ang).astype(np.float32)
        s = np.sin(ang).astype(np.float32)
        return np.concatenate([c, s, c * sc, s * sc], axis=1).astype(np.float32)

    ropep = tab(np.arange(T)).reshape(NCH, 128, 256)
    ropes = np.repeat(tab(np.array([16384])), NS, axis=0)
    lg = np.log1p(-np.exp2(-5.0 - np.arange(4, dtype=np.float32))).astype(np.float32)
    i = np.arange(128, dtype=np.float32)
    diff = i[:, None] - i[None, :]
    dmask = np.where(diff[None] >= 0, np.exp(np.maximum(diff, 0.0)[None] * lg[:, None, None]), 0.0).astype(np.float32)
    dmaskT = np.ascontiguousarray(dmask.transpose(2, 0, 1))
    q_dec = np.exp((i[:, None] + 1.0) * lg[None, :]).astype(np.float32)
    qdecT = np.ascontiguousarray(np.broadcast_to(q_dec.T[None], (128, 4, 128))).astype(np.float32)
    kdec = np.exp((127.0 - i)[:, None] * lg[None, :]).astype(np.float32)
    trilT = (i[:, None] <= i[None, :]).astype(np.float32)
    delta16 = np.ascontiguousarray(np.broadcast_to(np.eye(NS, dtype=np.float32)[None], (128, NS, NS)))
    deltaK = np.eye(NS, dtype=np.float32)
    return dict(identb=np.eye(128, dtype=np.float32).astype(ml_dtypes.bfloat16), identf=np.eye(128, dtype=np.float32),
                ropep=np.ascontiguousarray(ropep), ropes=np.ascontiguousarray(ropes), dmaskT=dmaskT, qdecT=qdecT,
                kdec=np.ascontiguousarray(kdec), trilT=np.ascontiguousarray(trilT), delta16=delta16, deltaK=deltaK)


_NC_CACHE = {}


def kernel(x_prompt, x_sample, state_ret, state_conv, c_prompt, c_sample, w_ada, b_ada, g_mix, w_in, ret_gn_gain,
           gmlp_ln_gain, w_s, b_s, w_out, g_ffn, w_up, conv_w, conv_b, w_down, g_final):
    f = lambda a: np.ascontiguousarray(np.asarray(a, dtype=np.float32))
    x_prompt, x_sample, state_ret, state_conv = f(x_prompt), f(x_sample), f(state_ret), f(state_conv)
    c_prompt, c_sample = f(c_prompt), f(c_sample)
    shared = dict(
        w_ada=f(w_ada), w_in=f(w_in), w_out=f(w_out), w_up=f(w_up), w_down=f(w_down),
        baT=f(f(b_ada).reshape(2, 6, 8, 128).transpose(3, 0, 1, 2)),
        gT=f(np.stack([f(g_mix)[0], f(g_mix)[1], f(g_ffn)[0], f(g_ffn)[1], f(g_final)]).reshape(5, 8, 128).transpose(2, 0, 1)),
        gfin=f(g_final).reshape(1, D), gng=f(ret_gn_gain), lng=f(gmlp_ln_gain),
        wsT=f(f(w_s).transpose(0, 3, 1, 2)),
        bsT=f(f(b_s).transpose(2, 0, 1)),
        ws00=f(f(w_s)[:, :, 0, 0]), bs0=f(f(b_s)[:, :, 0]),
        cwT=f(f(conv_w).reshape(2, 3, 32, 128).transpose(3, 0, 2, 1)),
        cbT=f(f(conv_b).reshape(2, 32, 128).transpose(2, 0, 1)),
    )
    shared.update(_consts())
    in_maps = []
    for c in range(NCORES):
        sl_ = slice(c * NS, (c + 1) * NS)
        m = dict(shared)
        m["xp"] = x_prompt[c]
        m["xs"] = f(x_sample[sl_, 0, :])
        m["cc"] = f(np.concatenate([c_prompt[c:c + 1], c_sample[sl_]], axis=0))
        m["sret"] = f(state_ret[:, sl_])
        m["sconvT"] = f(state_conv[:, sl_].reshape(2, NS, 2, 32, 128).transpose(4, 0, 2, 3, 1))
        in_maps.append(m)
    if "nc" not in _NC_CACHE:
        _NC_CACHE["nc"] = build_nc()
    import os
    ncr = int(os.environ.get("MK_CORES", str(NCORES)))
    res = run_bass_kernel_spmd(_NC_CACHE["nc"], in_maps[:ncr], core_ids=list(range(ncr)))
    R = list(res.results) + [res.results[0]] * (NCORES - ncr)
    y_prompt = np.stack([R[c]["yp"] for c in range(NCORES)]).astype(np.float32)
    y_sample = np.concatenate([R[c]["ys"] for c in range(NCORES)], axis=0).reshape(128, 1, D).astype(np.float32)
    new_ret_prompt = np.stack([R[c]["nrp"].transpose(0, 2, 1, 3) for c in range(NCORES)], axis=1).astype(np.float32)
    new_conv_prompt = np.stack([R[c]["ncpT"].transpose(1, 3, 2, 0).reshape(2, 2, 4096) for c in range(NCORES)], axis=1).astype(np.float32)
    new_ret_sample = np.concatenate([R[c]["nrs"] for c in range(NCORES)], axis=1).astype(np.float32)
    new_conv_sample = np.concatenate([R[c]["ncsT"].transpose(1, 4, 2, 3, 0).reshape(2, NS, 2, 4096) for c in range(NCORES)], axis=1).astype(np.float32)
    new_gmlp_v_sample = np.concatenate([R[c]["nvs"] for c in range(NCORES)], axis=1).reshape(2, 128, 1, 512).astype(np.float32)
    return (np.ascontiguousarray(y_prompt), np.ascontiguousarray(y_sample), np.ascontiguousarray(new_ret_prompt),
            np.ascontiguousarray(new_conv_prompt), np.ascontiguousarray(new_ret_sample),
            np.ascontiguousarray(new_conv_sample), np.ascontiguousarray(new_gmlp_v_sample))
```

```python
import contextlib
import math
import numpy as np
import ml_dtypes
import concourse.bass as bass
import concourse.mybir as mybir
from concourse.bass_utils import run_bass_kernel_spmd

F32 = mybir.dt.float32
BF16 = mybir.dt.bfloat16
AF = mybir.ActivationFunctionType
ALU = mybir.AluOpType
AX = mybir.AxisListType

ENGS = ("pe", "act", "dve", "pool", "sp")
NCORES = 8
T = 2048
NCH = 16
D = 1024
NS = 16
EPS = 1e-6
GAM = [1.0 - 2.0 ** (-5 - h) for h in range(4)]
import os as _os
POOL2DVE = _os.environ.get("MK_POOL2DVE", "0") == "1"


class Res:
    __slots__ = ("name", "writer", "readers")

    def __init__(self, name):
        self.name = name
        self.writer = None
        self.readers = []


class Op:
    __slots__ = ("eng", "fn", "dma_key", "deps", "signal", "ticket", "idx")

    def __init__(self, eng, fn, dma_key, idx):
        self.eng = eng
        self.fn = fn
        self.dma_key = dma_key
        self.deps = []
        self.signal = False
        self.ticket = None
        self.idx = idx


class Prog:
    def __init__(self, nc):
        self.nc = nc
        self.ops = []
        self.res = {}
        self.dma_count = {}

    def R(self, name):
        r = self.res.get(name)
        if r is None:
            r = Res(name)
            self.res[name] = r
        return r

    def op(self, eng, fn, reads=(), writes=(), dma_key=None, keep=False):
        if POOL2DVE and eng == "pool" and dma_key is None and not keep:
            eng = "dve"
        o = Op(eng, fn, dma_key, len(self.ops))
        deps = {}
        is_dma = dma_key is not None

        def add(d):
            if d is None or d is o:
                return
            deps[d.idx] = d

        rs = [self.R(r) for r in reads]
        ws = [self.R(w) for w in writes]
        for r in rs:
            add(r.writer)
        for w in ws:
            pw = w.writer
            if pw is not None:
                same_eng_compute = (pw.eng == eng == "pe" and pw.dma_key is None and not is_dma)
                same_key_dma = (is_dma and pw.dma_key == dma_key)
                if not (same_eng_compute or same_key_dma):
                    add(pw)
            for rd in w.readers:
                if rd.eng == eng == "pe" and rd.dma_key is None and not is_dma:
                    continue
                add(rd)
        for r in rs:
            r.readers.append(o)
        for w in ws:
            w.writer = o
            w.readers = []
        if is_dma:
            self.dma_count[dma_key] = self.dma_count.get(dma_key, 0) + 1
        latest = {}
        for d in deps.values():
            k = ("dma", d.dma_key) if d.dma_key is not None else ("eng", d.eng)
            if k not in latest or latest[k].idx < d.idx:
                latest[k] = d
        for d in latest.values():
            if d.dma_key is not None:
                o.deps.append((d, self.dma_count[d.dma_key] * 16))
            else:
                o.deps.append((d, None))
                d.signal = True
        self.ops.append(o)
        return o

    def emit(self):
        nc = self.nc
        counters = {e: 0 for e in ENGS}
        for o in self.ops:
            if o.dma_key is None and o.signal:
                counters[o.eng] += 1
                o.ticket = counters[o.eng]
        keys = sorted(self.dma_count.keys())
        self.stats = {e: (sum(1 for o in self.ops if o.eng == e), counters[e]) for e in ENGS}
        with contextlib.ExitStack() as st:
            esem = {e: st.enter_context(nc.semaphore("s_" + e)) for e in ENGS if e != "sp"}
            dsem = {k: st.enter_context(nc.semaphore("d_" + str(k))) for k in keys}
            block = st.enter_context(nc.Block())
            per_eng = {e: [o for o in self.ops if o.eng == e] for e in ENGS}

            def run(engname, engobj):
                waited = {}
                for o in per_eng[engname]:
                    need = {}
                    for d, val in o.deps:
                        if d.dma_key is not None:
                            s = dsem[d.dma_key]
                            v = val
                        else:
                            s = esem[d.eng]
                            v = d.ticket
                        key = id(s)
                        if v > waited.get(key, 0) and v > need.get(key, (None, 0))[1]:
                            need[key] = (s, v)
                    for key, (s, v) in need.items():
                        engobj.wait_ge(s, v)
                        waited[key] = v
                    ins = o.fn(engobj)
                    if o.dma_key is not None:
                        ins.then_inc(dsem[o.dma_key], 16)
                    elif o.signal:
                        ins.then_inc(esem[o.eng], 1)
                if engname == "sp":
                    for k in keys:
                        engobj.wait_ge(dsem[k], self.dma_count[k] * 16)

            @block.sync
            def _(e):
                run("sp", e)

            @block.scalar
            def _(e):
                run("act", e)

            @block.vector
            def _(e):
                run("dve", e)

            @block.gpsimd
            def _(e):
                run("pool", e)

            @block.tensor
            def _(e):
                run("pe", e)


def build_nc():
    nc = bass.Bass("TRN2", target_bir_lowering=False)

    def din(name, shape, dt=F32):
        return nc.dram_tensor(name, list(shape), dt, kind="ExternalInput").ap()

    def dout(name, shape, dt=F32):
        return nc.dram_tensor(name, list(shape), dt, kind="ExternalOutput").ap()

    xp = din("xp", [T, D])
    xs = din("xs", [NS, D])
    cc = din("cc", [NS + 1, D])
    sret = din("sret", [2, NS, 4, 128, 128])
    sconvT = din("sconvT", [128, 2, 2, 32, NS])
    w_ada = din("w_ada", [2, D, 6 * D])
    w_in = din("w_in", [2, D, 3072])
    w_out = din("w_out", [2, D, D])
    w_up = din("w_up", [2, D, 4096])
    w_down = din("w_down", [2, 2048, D])
    baT = din("baT", [128, 2, 6, 8])
    gT = din("gT", [128, 5, 8])
    gfin = din("gfin", [1, D])
    gng = din("gng", [2, 512])
    lng = din("lng", [2, 512])
    wsT = din("wsT", [2, 128, 4, 128])
    bsT = din("bsT", [128, 2, 4])
    ws00 = din("ws00", [2, 4])
    bs0 = din("bs0", [2, 4])
    cwT = din("cwT", [128, 2, 32, 3])
    cbT = din("cbT", [128, 2, 32])
    identb_d = din("identb", [128, 128], BF16)
    identf_d = din("identf", [128, 128])
    ropep = din("ropep", [NCH, 128, 256])
    ropes = din("ropes", [NS, 256])
    dmaskT_d = din("dmaskT", [128, 4, 128])
    qdecT_d = din("qdecT", [128, 4, 128])
    kdec_d = din("kdec", [128, 4])
    trilT_d = din("trilT", [128, 128])
    delta16_d = din("delta16", [128, NS, NS])
    deltaK_d = din("deltaK", [NS, NS])

    yp = dout("yp", [T, D])
    ys = dout("ys", [NS, D])
    nrp = dout("nrp", [2, 128, 4, 128])
    ncpT = dout("ncpT", [128, 2, 32, 2])
    nrs = dout("nrs", [2, NS, 4, 128, 128])
    ncsT = dout("ncsT", [128, 2, 2, 32, NS])
    nvs = dout("nvs", [2, NS, 512])

    with contextlib.ExitStack() as st:
        def sb(name, shape, dt=F32):
            return st.enter_context(nc.sbuf_tensor(name, list(shape), dt))

        def psb(name):
            return st.enter_context(nc.psum_tensor(name, [128, 512], F32))

        P = Prog(nc)
        op = P.op

        xres = sb("xres", [128, NCH, D])
        arena = sb("arena", [128, 49152], BF16)
        xs_t = sb("xs_t", [128, D])
        xn = sb("xn", [128, D], BF16)
        tmp = sb("tmp", [128, D])
        hT2 = sb("hT2", [128, 8, 256], BF16)
        hT = hT2[:, :, 0:128]
        stt = sb("stt", [128, 64])
        PH = sb("PH", [128, 4624])
        hist = sb("hist", [128, 32, 2])
        modT = sb("modT", [128, 2, 6, 8, NS + 1])
        baT_t = sb("baT_t", [128, 2, 6, 8])
        gT_t = sb("gT_t", [128, 5, 8])
        GTp = sb("GTp", [128, D])
        cT = sb("cT", [128, 8, NS + 1], BF16)
        identb = sb("identb_t", [128, 128], BF16)
        identf = sb("identf_t", [128, 128])
        mh = sb("mh", [128, 4])
        kdec = sb("kdec_t", [128, 4])
        deltaK = sb("deltaK_t", [128, NS])
        wsS_b = sb("wsS_b", [128, 4, NS], BF16)
        bsT_t = sb("bsT_t", [128, 2, 4])
        w00 = sb("w00", [128, 4])
        b00 = sb("b00", [128, 4])
        cw = sb("cw", [128, 2, 32, 3])
        cb = sb("cb", [128, 2, 32])

        def phv(off, n, dt=F32, pat=None, **kw):
            a = PH[:, off:off + n]
            if dt == BF16:
                a = a.bitcast(BF16)
            if pat:
                a = a.rearrange(pat, **kw)
            return a

        ropeb = phv(0, 512, F32, "p (s c) -> p s c", s=2)
        ropes_t = phv(512, 256)
        dmaskT = phv(768, 512, F32, "p (h t) -> p h t", h=4)
        qdecT = phv(1280, 512, F32, "p (h t) -> p h t", h=4)
        trilT = phv(1792, 128)
        delta16 = phv(1920, 256, F32, "p (b m) -> p b m", b=NS)
        gn_tab = phv(2176, 512)
        ln_tab = phv(2688, 512)
        wsT_b = phv(3200, 256, BF16, "p (g t) -> p g t", g=4)
        S_f = phv(3456, 512, F32, "p (h t) -> p h t", h=4)
        S_b = phv(3968, 256, BF16, "p (h t) -> p h t", h=4)
        fT = phv(0, 2048, BF16, "p (j t) -> p j t", j=16)
        U = phv(2048, 1040, F32, "p (a s t) -> p a s t", a=2, s=2)
        tb = phv(3088, 1024, F32, "p (a s t) -> p a s t", a=2, s=2)
        sl = phv(4112, 512, F32, "p (a t) -> p a t", a=2)
        scT0 = phv(2048 + 520, 512, F32, "p (m b) -> p m b", m=32)
        scT1 = phv(3088 + 512, 512, F32, "p (m b) -> p m b", m=32)
        upTs = tmp[:, 0:512].rearrange("p (m b) -> p m b", m=32)
        PH1_NAMES = ["ropeb0", "ropeb1", "ropes_t", "dmaskT", "qdecT", "trilT", "delta16", "gn_tab", "ln_tab", "wsT_b", "S_f", "S_b", "q2T"]
        PH2_NAMES = ["fT", "U00", "U01", "U10", "U11", "tb00", "tb01", "tb10", "tb11", "sl0", "sl1"]

        bank = [psb("bank%d" % i) for i in range(8)]
        pT = bank[0][:, 0:512].bitcast(BF16).rearrange("p (k t) -> p k t", k=8)
        pT2 = bank[1][:, 0:512].bitcast(BF16).rearrange("p (k t) -> p k t", k=8)
        pTf = [bank[0], bank[1]]
        pp = bank[2:8]
        PPN = ["pp%d" % i for i in range(6)]

        def aview(off, n, dt, pat=None, **kw):
            a = arena[:, off:off + n]
            if dt == F32:
                a = a.bitcast(F32)
            if pat:
                a = a.rearrange(pat, **kw)
            return a

        wa = [aview(0, 8192, BF16, "p (k n) -> p k n", k=8), aview(8192, 8192, BF16, "p (k n) -> p k n", k=8)]
        Win = aview(0, 24576, BF16, "p (k n) -> p k n", k=8)
        Wout = aview(24576, 8192, BF16, "p (k n) -> p k n", k=8)
        Wup = aview(0, 32768, BF16, "p (k n) -> p k n", k=8)
        Wdn = aview(32768, 16384, BF16, "p (k n) -> p k n", k=16)
        AU = 32768
        qr = [aview(AU + 0, 512, BF16), aview(AU + 1536, 512, BF16)]
        kr = [aview(AU + 512, 512, BF16), aview(AU + 2048, 512, BF16)]
        vb = [aview(AU + 1024, 512, BF16), aview(AU + 2560, 512, BF16)]
        kk = aview(AU + 3072, 512, BF16)
        sg = [aview(AU + 3584, 1024, F32), aview(AU + 4608, 1024, F32)]
        gu = [aview(AU + 5632, 1024, F32), aview(AU + 6656, 1024, F32)]
        gv = aview(AU + 7680, 1024, F32)
        vn = [aview(AU + 8704, 512, BF16), aview(AU + 9216, 512, BF16)]
        t1 = aview(AU + 9728, 512, F32)
        t2 = aview(AU + 10240, 512, F32)
        t12 = aview(AU + 9728, 1024, F32)
        hx = aview(AU + 10752, 1024, F32)
        gA = aview(AU + 11776, 1024, F32)
        cen = aview(AU + 12800, 1024, F32)
        qkT = aview(AU + 13824, 1024, BF16, "p (k t) -> p k t", k=8)
        mixinT = qkT
        mixin = aview(AU + 14848, 1024, BF16)
        attTm = aview(AU + 15872, 512, BF16, "p (k t) -> p k t", k=4)
        q2T = phv(4224, 256, BF16, "p (k t) -> p k t", k=4)
        vnf = sg[1]
        Sg_f = aview(AU + 10752, 2048, F32, "p (b h e) -> p b h e", b=2, h=4)
        Sg_b = aview(AU + 6656, 1024, BF16, "p (b h e) -> p b h e", b=2, h=4)
        Km = aview(AU + 1536, 1024, BF16, "p (b n) -> p b n", b=2)
        o1 = cen
        o_s = gv
        Qm = aview(AU + 14848, 1024, BF16, "p (h b m) -> p h b m", h=4, b=NS)
        AU_NAMES = ["qr0", "kr0", "vb0", "qr1", "kr1", "vb1", "kk", "sg0", "sg1", "gu0", "gu1", "gv", "vn0", "vn1",
                    "t1", "t2", "hx", "gA", "cen", "qkT", "mixin", "attTm"]

        def ld(dst, src, name, eng="sp", key="c"):
            op(eng, lambda e: e.dma_start(out=dst, in_=src), writes=[name], dma_key=key)

        ld(identb[:], identb_d, "identb")
        ld(identf[:], identf_d, "identf")
        ld(tmp[0:NS + 1, :], cc, "tmp")
        ld(baT_t[:], baT, "baT_t")
        ld(gT_t[:], gT, "gT_t")
        ld(xs_t[0:NS, :], xs, "xs_t")
        ld(kdec[:], kdec_d, "kdec")
        ld(deltaK[0:NS, :], deltaK_d, "deltaK")
        ld(bsT_t[:], bsT, "bsT_t")
        ld(cw[:], cwT, "cw")
        ld(cb[:], cbT, "cb")
        op("pool", lambda e: e.memset(mh[:], -0.5), writes=["mh"])
        for q4 in range(4):
            op("act", lambda e, q4=q4: e.dma_start(
                out=xres[:, q4 * 4:(q4 + 1) * 4, :],
                in_=xp[q4 * 512:(q4 + 1) * 512, :].rearrange("(c p) f -> p c f", p=128)),
               writes=["x%d" % c for c in range(q4 * 4, q4 * 4 + 4)], dma_key="x%d" % q4)

        op("act", lambda e: e.activation(out=xn[0:NS + 1, :], in_=tmp[0:NS + 1, :], func=AF.Silu),
           reads=["tmp"], writes=["xn"])
        for k in range(8):
            op("pe", lambda e, k=k: e.transpose(out=pT[:, k, 0:NS + 1], in_=xn[0:NS + 1, k * 128:(k + 1) * 128],
                                                identity=identb[0:NS + 1, 0:NS + 1]),
               reads=["xn", "identb"], writes=["pT"])
        op("dve", lambda e: e.tensor_copy(out=cT[:], in_=pT[:, :, 0:NS + 1]), reads=["pT"], writes=["cT"])
        ji = 0
        for l in range(2):
            for v in range(6):
                b = ji % 2
                wname = "wa%d" % b
                op("pool", lambda e, l=l, v=v, b=b: e.dma_start(
                    out=wa[b], in_=w_ada[l, :, v * D:(v + 1) * D].rearrange("(k p) n -> p k n", p=128)),
                   writes=[wname], dma_key=wname)
                pbank = pp[ji % 2]
                pname = PPN[ji % 2]
                pv = pbank[:, 0:8 * (NS + 1)].rearrange("p (m t) -> p m t", m=8)
                for m in range(8):
                    for k in range(8):
                        op("pe", lambda e, b=b, m=m, k=k, pv=pv: e.matmul(
                            pv[:, m, :], lhsT=wa[b][:, k, m * 128:(m + 1) * 128], rhs=cT[:, k, :],
                            start=(k == 0), stop=(k == 7)),
                           reads=[wname, "cT"], writes=[pname])
                op("dve", lambda e, l=l, v=v, pv=pv: e.tensor_tensor(
                    out=modT[:, l, v, :, :], in0=pv,
                    in1=baT_t[:, l, v, :].unsqueeze(2).to_broadcast([128, 8, NS + 1]), op=ALU.add),
                   reads=[pname, "baT_t"], writes=["modT"])
                if v in (1, 4):
                    gi = l if v == 1 else 2 + l
                    op("dve", lambda e, l=l, v=v, gi=gi: e.scalar_tensor_tensor(
                        out=modT[:, l, v, :, :], in0=modT[:, l, v, :, :], scalar=1.0,
                        in1=gT_t[:, gi, :].unsqueeze(2).to_broadcast([128, 8, NS + 1]),
                        op0=ALU.add, op1=ALU.mult),
                       reads=["modT", "gT_t"], writes=["modT"])
                ji += 1

        import os
        STOP = int(os.environ.get("MK_STOP", "-1"))

        class _Stop(Exception):
            pass

        def chk(n):
            if STOP == n:
                raise _Stop()

        def build_gt(l, v, sample):
            if not sample:
                op("dve", lambda e: e.tensor_copy(
                    out=tmp[:].rearrange("p (k t) -> p k t", k=8),
                    in_=modT[:, l, v, :, 0:1].to_broadcast([128, 8, 128])),
                   reads=["modT"], writes=["tmp"])
                for k in range(8):
                    bk = pTf[k // 4]
                    op("pe", lambda e, k=k, bk=bk: e.transpose(
                        out=bk[:, (k % 4) * 128:(k % 4 + 1) * 128], in_=tmp[:, k * 128:(k + 1) * 128], identity=identf[:]),
                       reads=["tmp", "identf"], writes=["pT" if k < 4 else "pT2"])
                op("act", lambda e: e.copy(out=GTp[:, 0:512], in_=pTf[0][:]), reads=["pT"], writes=["GTp"])
                op("act", lambda e: e.copy(out=GTp[:, 512:1024], in_=pTf[1][:]), reads=["pT2"], writes=["GTp"])
            else:
                for k in range(8):
                    bk = pTf[k // 4]
                    op("pe", lambda e, k=k, bk=bk: e.transpose(
                        out=bk[0:NS, (k % 4) * 128:(k % 4 + 1) * 128], in_=modT[:, l, v, k, 1:NS + 1], identity=identf[:]),
                       reads=["modT", "identf"], writes=["pT" if k < 4 else "pT2"])
                op("act", lambda e: e.copy(out=GTp[0:NS, 0:512], in_=pTf[0][0:NS, :]), reads=["pT"], writes=["GTp"])
                op("act", lambda e: e.copy(out=GTp[0:NS, 512:1024], in_=pTf[1][0:NS, :]), reads=["pT2"], writes=["GTp"])

        def norm_mod(xc, xname, Pn, l, vsh, vsc, sample, hdst=None, hname="hT0", sc=0):
            if hdst is None:
                hdst = hT
            op("act", lambda e: e.activation(out=xn[0:Pn, :], in_=xc, func=AF.Square, scale=1.0 / 32.0,
                                             accum_out=stt[0:Pn, sc:sc + 1]),
               reads=[xname], writes=["xn", "sttn%d" % sc])
            op("dve", lambda e: e.tensor_scalar(out=stt[0:Pn, sc:sc + 1], in0=stt[0:Pn, sc:sc + 1], scalar1=EPS, scalar2=None,
                                                op0=ALU.add), reads=["sttn%d" % sc], writes=["sttn%d" % sc])
            op("pool", lambda e: e.tensor_tensor(out=stt[0:Pn, sc + 1:sc + 2], in0=stt[0:Pn, sc:sc + 1], in1=mh[0:Pn, 0:1], op=ALU.pow),
               reads=["sttn%d" % sc, "mh"], writes=["sttn%d" % sc], keep=True)
            op("act", lambda e: e.activation(out=xn[0:Pn, :], in_=xc, func=AF.Copy, scale=stt[0:Pn, sc + 1:sc + 2]),
               reads=[xname, "sttn%d" % sc], writes=["xn"])
            for k in range(8):
                op("pe", lambda e, k=k: e.transpose(out=pT[:, k, 0:Pn], in_=xn[0:Pn, k * 128:(k + 1) * 128],
                                                    identity=identb[0:Pn, 0:Pn]),
                   reads=["xn", "identb"], writes=["pT"])
            if sample:
                gtab = modT[:, l, vsc, :, 1:NS + 1]
                stab = modT[:, l, vsh, :, 1:NS + 1]
                t3 = tmp[:].rearrange("p (k t) -> p k t", k=8)[:, :, 0:Pn]
                op("dve", lambda e: e.tensor_tensor(out=t3, in0=pT[:, :, 0:Pn], in1=gtab, op=ALU.mult),
                   reads=["pT", "modT"], writes=["tmp"])
                op("pool", lambda e: e.tensor_tensor(out=hdst[:, :, 0:Pn], in0=t3, in1=stab, op=ALU.add),
                   reads=["tmp", "modT"], writes=[hname])
            else:
                for k in range(8):
                    op("act", lambda e, k=k: e.activation(out=hdst[:, k, 0:Pn], in_=pT[:, k, 0:Pn], func=AF.Identity,
                                                          scale=modT[:, l, vsc, k, 0:1], bias=modT[:, l, vsh, k, 0:1]),
                       reads=["pT", "modT"], writes=[hname])

        def group_norm(src, srcname, Pn, gain, gname, out, oname, cbuf, cnames, sbuf_, snames, col0, stn):
            m = stt[0:Pn, col0:col0 + 4]
            vv_ = stt[0:Pn, col0 + 4:col0 + 8]
            r = stt[0:Pn, col0 + 8:col0 + 12]
            cen3 = cbuf[0:Pn, :].rearrange("p (g c) -> p g c", g=4)
            sq3 = sbuf_[0:Pn, :].rearrange("p (g c) -> p g c", g=4)
            op("dve", lambda e: e.tensor_reduce(out=m, in_=src, axis=AX.X, op=ALU.add), reads=[srcname], writes=[stn])
            op("dve", lambda e: e.tensor_scalar(out=m, in0=m, scalar1=1.0 / 128.0, scalar2=None, op0=ALU.mult),
               reads=[stn], writes=[stn])
            op("dve", lambda e: e.tensor_tensor(out=cen3, in0=src, in1=m.unsqueeze(2).to_broadcast([Pn, 4, 128]),
                                                op=ALU.subtract), reads=[srcname, stn], writes=cnames)
            op("pool", lambda e: e.tensor_tensor(out=sq3, in0=cen3, in1=cen3, op=ALU.mult),
               reads=cnames, writes=snames)
            op("dve", lambda e: e.tensor_reduce(out=vv_, in_=sq3, axis=AX.X, op=ALU.add),
               reads=snames, writes=[stn])
            op("dve", lambda e: e.tensor_scalar(out=vv_, in0=vv_, scalar1=1.0 / 128.0, scalar2=EPS, op0=ALU.mult,
                                                op1=ALU.add), reads=[stn], writes=[stn])
            op("pool", lambda e: e.tensor_tensor(out=r, in0=vv_, in1=mh[0:Pn, :], op=ALU.pow),
               reads=[stn, "mh"], writes=[stn], keep=True)
            op("dve", lambda e: e.tensor_tensor(out=cen3, in0=cen3, in1=r.unsqueeze(2).to_broadcast([Pn, 4, 128]),
                                                op=ALU.mult), reads=cnames + [stn], writes=cnames)
            op("pool", lambda e: e.tensor_tensor(out=out, in0=cbuf[0:Pn, :], in1=gain[0:Pn, :], op=ALU.mult),
               reads=cnames + [gname], writes=[oname])

        def gelu(src, srcname, Pn, out, oname):
            op("act", lambda e: e.activation(out=hx[0:Pn, :], in_=src, func=AF.Copy, scale=0.5),
               reads=[srcname], writes=["hx"])
            op("act", lambda e: e.activation(out=gA[0:Pn, :], in_=src, func=AF.Square), reads=[srcname], writes=["gA"])
            op("dve", lambda e: e.tensor_scalar(out=gA[0:Pn, :], in0=gA[0:Pn, :], scalar1=0.044715, scalar2=1.0,
                                                op0=ALU.mult, op1=ALU.add), reads=["gA"], writes=["gA"])
            op("dve", lambda e: e.tensor_tensor(out=gA[0:Pn, :], in0=gA[0:Pn, :], in1=hx[0:Pn, :], op=ALU.mult),
               reads=["gA", "hx"], writes=["gA"])
            op("act", lambda e: e.activation(out=gA[0:Pn, :], in_=gA[0:Pn, :], func=AF.Tanh,
                                             scale=2.0 * math.sqrt(2.0 / math.pi)), reads=["gA"], writes=["gA"])
            op("dve", lambda e: e.scalar_tensor_tensor(out=out, in0=gA[0:Pn, :], scalar=1.0, in1=hx[0:Pn, :],
                                                       op0=ALU.add, op1=ALU.mult), reads=["gA", "hx"], writes=[oname])

        def rope(src, srcname, Pn, tab, tabname, coff, out, oname):
            s3 = src.rearrange("p (h d) -> p h d", h=4)
            x1 = s3[:, :, 0:64]
            x2 = s3[:, :, 64:128]
            cosb = tab[:, coff:coff + 64].unsqueeze(1).to_broadcast([Pn, 4, 64])
            sinb = tab[:, coff + 64:coff + 128].unsqueeze(1).to_broadcast([Pn, 4, 64])
            a = t1[0:Pn, :].rearrange("p (h d) -> p h d", h=4)
            b_ = t2[0:Pn, :].rearrange("p (h d) -> p h d", h=4)
            o3 = out.rearrange("p (h d) -> p h d", h=4)
            op("dve", lambda e: e.tensor_tensor(out=a, in0=x1, in1=cosb, op=ALU.mult), reads=[srcname, tabname], writes=["t1"])
            op("dve", lambda e: e.tensor_tensor(out=b_, in0=x2, in1=sinb, op=ALU.mult), reads=[srcname, tabname], writes=["t2"])
            op("pool", lambda e: e.tensor_tensor(out=o3[:, :, 0:64], in0=a, in1=b_, op=ALU.subtract),
               reads=["t1", "t2"], writes=[oname])
            op("dve", lambda e: e.tensor_tensor(out=a, in0=x1, in1=sinb, op=ALU.mult), reads=[srcname, tabname, oname], writes=["t1"])
            op("dve", lambda e: e.tensor_tensor(out=b_, in0=x2, in1=cosb, op=ALU.mult), reads=[srcname, tabname, oname], writes=["t2"])
            op("pool", lambda e: e.tensor_tensor(out=o3[:, :, 64:128], in0=a, in1=b_, op=ALU.add),
               reads=["t1", "t2"], writes=[oname])

        DBG = {}

        def mix_ctx(c, sample):
            par = 0 if sample else c % 2
            Pn = NS if sample else 128
            if sample:
                return par, Pn, xs_t[0:NS, :], "xs_t", ropes_t[0:NS, :], "ropes_t"
            return par, Pn, xres[:, c, :], "x%d" % c, ropeb[:, c % 2, :], "ropeb%d" % (c % 2)

        def mix_N(l, c, sample):
            par, Pn, xc, xname, rtab, rname = mix_ctx(c, sample)
            if not sample:
                op("sp", lambda e: e.dma_start(out=rtab, in_=ropep[c]),
                   writes=[rname] + (PH2_NAMES if c < 2 else []), dma_key=rname)
            norm_mod(xc, xname, Pn, l, 0, 1, sample, hdst=hT2[:, :, par * 128:(par + 1) * 128], hname="hT%d" % par, sc=2 * par)

        def mix_PA(l, c, sample):
            par, Pn, xc, xname, rtab, rname = mix_ctx(c, sample)
            for n in range(3):
                for k in range(8):
                    op("pe", lambda e, n=n, k=k: e.matmul(pp[n][0:Pn, :], lhsT=hT2[:, k, par * 128:par * 128 + Pn],
                                                           rhs=Win[:, k, n * 512:(n + 1) * 512],
                                                           start=(k == 0), stop=(k == 7)),
                       reads=["hT%d" % par, "w_in"], writes=[PPN[n]])

        def mix_EA(l, c, sample):
            par, Pn, xc, xname, rtab, rname = mix_ctx(c, sample)
            rope(pp[0][0:Pn, :], PPN[0], Pn, rtab, rname, 0, qr[par][0:Pn, :], "qr%d" % par)
            rope(pp[1][0:Pn, :], PPN[1], Pn, rtab, rname, 128, kr[par][0:Pn, :], "kr%d" % par)
            op("act", lambda e: e.copy(out=vb[par][0:Pn, :], in_=pp[2][0:Pn, :]), reads=[PPN[2]], writes=["vb%d" % par])

        def mix_PB(l, c, sample):
            par, Pn, xc, xname, rtab, rname = mix_ctx(c, sample)
            for n in range(3):
                for k in range(8):
                    op("pe", lambda e, n=n, k=k: e.matmul(pp[n][0:Pn, :], lhsT=hT2[:, k, par * 128:par * 128 + Pn],
                                                           rhs=Win[:, k, (3 + n) * 512:(4 + n) * 512],
                                                           start=(k == 0), stop=(k == 7)),
                       reads=["hT%d" % par, "w_in"], writes=[PPN[n]])

        def mix_EB(l, c, sample):
            par, Pn, xc, xname, rtab, rname = mix_ctx(c, sample)
            op("act", lambda e: e.activation(out=sg[par][0:Pn, :], in_=pp[0][0:Pn, :], func=AF.Silu),
               reads=[PPN[0]], writes=["sg%d" % par])
            gelu(pp[1][0:Pn, :], PPN[1], Pn, gu[par][0:Pn, :], "gu%d" % par)
            gelu(pp[2][0:Pn, :], PPN[2], Pn, gv[0:Pn, :], "gv")
            gv3 = gv[0:Pn, :].rearrange("p (g c) -> p g c", g=4)
            if sample:
                group_norm(gv3, "gv", Pn, ln_tab, "ln_tab", vnf[0:Pn, :], "sg1", hx, ["hx"], gA, ["gA"], 16, "sttf")
                op("act", lambda e: e.copy(out=vn[0][0:Pn, :], in_=vnf[0:Pn, :]), reads=["sg1"], writes=["vn0"])
                op("sp", lambda e: e.dma_start(out=nvs[l], in_=vnf[0:NS, :]), reads=["sg1"], dma_key="o_nvs")
            else:
                group_norm(gv3, "gv", Pn, ln_tab, "ln_tab", vn[par][0:Pn, :], "vn%d" % par, hx, ["hx"], gA, ["gA"], 16, "sttf")

        def mix_back(l, c, sample, h1=None, h2=None):
            par, Pn, xc, xname, rtab, rname = mix_ctx(c, sample)
            qrn, krn, vbn, sgn, gun, vnn = ["%s%d" % (b_, par) for b_ in ("qr", "kr", "vb", "sg", "gu", "vn")]
            qr_, kr_, vb_, sg_, gu_, vn_ = qr[par], kr[par], vb[par], sg[par], gu[par], vn[par]
            if not sample:
                op("dve", lambda e: e.tensor_tensor(out=kk[:].rearrange("p (h d) -> p h d", h=4),
                                                    in0=kr_[:].rearrange("p (h d) -> p h d", h=4),
                                                    in1=kdec[:].unsqueeze(2).to_broadcast([128, 4, 128]), op=ALU.mult),
                   reads=[krn, "kdec"], writes=["kk"])
                if l == 0 and c == DBG.get('lastc') and not sample:
                    chk(310)
                for h in range(4):
                    op("pe", lambda e, h=h: e.transpose(out=pT2[:, h, :], in_=qr_[:, h * 128:(h + 1) * 128], identity=identb[:]),
                       reads=[qrn, "identb"], writes=["pT2"])
                for h in range(4):
                    op("pe", lambda e, h=h: e.transpose(out=pT2[:, 4 + h, :], in_=kr_[:, h * 128:(h + 1) * 128], identity=identb[:]),
                       reads=[krn, "identb"], writes=["pT2"])
                if l == 0 and c == DBG.get('lastc') and not sample:
                    chk(311)
                op("act", lambda e: e.copy(out=qkT[:], in_=pT2[:]), reads=["pT2"], writes=["qkT"])
                op("dve", lambda e: e.tensor_tensor(out=q2T, in0=qkT[:, 0:4, :], in1=qdecT, op=ALU.mult),
                   reads=["qkT", "qdecT"], writes=["q2T"])
                if l == 0 and c == DBG.get('lastc') and not sample:
                    chk(312)
                pa = pp[3][:].rearrange("p (h t) -> p h t", h=4)
                for h in range(4):
                    op("pe", lambda e, h=h: e.matmul(pa[:, h, :], lhsT=qkT[:, 4 + h, :], rhs=qkT[:, h, :], start=True, stop=True),
                       reads=["qkT"], writes=[PPN[3]])
                if l == 0 and c == DBG.get('lastc') and not sample:
                    chk(313)
                op("dve", lambda e: e.tensor_tensor(out=attTm[:], in0=pa, in1=dmaskT, op=ALU.mult),
                   reads=[PPN[3], "dmaskT"], writes=["attTm"])
                if l == 0 and c == DBG.get('lastc') and not sample:
                    chk(314)
                po = pp[4][:].rearrange("p (h t) -> p h t", h=4)
                for h in range(4):
                    op("pe", lambda e, h=h: e.matmul(po[:, h, :], lhsT=attTm[:, h, :], rhs=vb_[:, h * 128:(h + 1) * 128],
                                                      start=True, stop=False), reads=["attTm", vbn], writes=[PPN[4]])
                    op("pe", lambda e, h=h: e.matmul(po[:, h, :], lhsT=q2T[:, h, :], rhs=S_b[:, h, :],
                                                      start=False, stop=True), reads=["q2T", "S_b"], writes=[PPN[4]])
                if l == 0 and c == DBG.get('lastc') and not sample:
                    chk(315)
                pS = pp[5][:].rearrange("p (h t) -> p h t", h=4)
                for h in range(4):
                    op("pe", lambda e, h=h: e.matmul(pS[:, h, :], lhsT=kk[:, h * 128:(h + 1) * 128], rhs=vb_[:, h * 128:(h + 1) * 128],
                                                      start=True, stop=True), reads=["kk", vbn], writes=[PPN[5]])
                for h in range(4):
                    op("dve", lambda e, h=h: e.scalar_tensor_tensor(out=S_f[:, h, :], in0=S_f[:, h, :], scalar=GAM[h] ** 128,
                                                                    in1=pS[:, h, :], op0=ALU.mult, op1=ALU.add),
                       reads=["S_f", PPN[5]], writes=["S_f"])
                op("act", lambda e: e.copy(out=S_b, in_=S_f), reads=["S_f"], writes=["S_b"])
                osrc, osname = po, PPN[4]
            else:
                pr3 = t12[0:NS, :]
                op("dve", lambda e: e.tensor_tensor(out=pr3, in0=qr_[0:NS, :], in1=kr_[0:NS, :], op=ALU.mult),
                   reads=[qrn, krn], writes=["t1", "t2"])
                op("dve", lambda e: e.tensor_reduce(out=stt[0:NS, 28:32], in_=pr3.rearrange("p (h d) -> p h d", h=4),
                                                    axis=AX.X, op=ALU.add), reads=["t1", "t2"], writes=["sttq"])
                op("dve", lambda e: e.tensor_tensor(out=o1[0:NS, :].rearrange("p (h d) -> p h d", h=4),
                                                    in0=vb_[0:NS, :].rearrange("p (h d) -> p h d", h=4),
                                                    in1=stt[0:NS, 28:32].unsqueeze(2).to_broadcast([NS, 4, 128]), op=ALU.mult),
                   reads=[vbn, "sttq"], writes=["cen"])
                for h in range(4):
                    op("pe", lambda e, h=h: e.transpose(out=pT2[:, h, 0:NS], in_=qr_[0:NS, h * 128:(h + 1) * 128],
                                                        identity=identb[0:NS, 0:NS]), reads=[qrn, "identb"], writes=["pT2"])
                op("act", lambda e: e.copy(out=qkT[:, 0:4, 0:NS], in_=pT2[:, 0:4, 0:NS]), reads=["pT2"], writes=["qkT"])
                op("dve", lambda e: e.tensor_tensor(
                    out=Qm[:], in0=qkT[:, 0:4, 0:NS].unsqueeze(3).to_broadcast([128, 4, NS, NS]),
                    in1=delta16.unsqueeze(1).to_broadcast([128, 4, NS, NS]), op=ALU.mult),
                   reads=["qkT", "delta16"], writes=["mixin"])
                for g2 in range(NS // 2):
                    op("sp", lambda e, g2=g2: e.dma_start(
                        out=Sg_f[:], in_=sret[l, g2 * 2:(g2 + 1) * 2].rearrange("b h d e -> d b h e")),
                       writes=["hx", "gA"], dma_key="sgf")
                    op("pool", lambda e, g2=g2: e.dma_start(
                        out=Sg_b[:], in_=sret[l, g2 * 2:(g2 + 1) * 2].rearrange("b h d e -> d b h e")),
                       writes=["gu1"], dma_key="sgb")
                    op("dve", lambda e, g2=g2: e.tensor_tensor(
                        out=Km[0:NS, :, :], in0=kr_[0:NS, :].unsqueeze(1).to_broadcast([NS, 2, 512]),
                        in1=deltaK[0:NS, g2 * 2:(g2 + 1) * 2].unsqueeze(2).to_broadcast([NS, 2, 512]), op=ALU.mult),
                       reads=[krn, "deltaK"], writes=["qr1", "kr1"])
                    for bl in range(2):
                        bq = g2 * 2 + bl
                        for h in range(4):
                            op("pe", lambda e, bl=bl, bq=bq, h=h: e.matmul(
                                pp[h][0:NS, 0:128], lhsT=Qm[:, h, bq, :], rhs=Sg_b[:, bl, h, :],
                                start=(bq == 0), stop=(bq == NS - 1)),
                               reads=["mixin", "gu1"], writes=[PPN[h]])
                    pSg = [pp[4][:].rearrange("p (h t) -> p h t", h=4), pp[5][:].rearrange("p (h t) -> p h t", h=4)]
                    for bl in range(2):
                        for h in range(4):
                            op("pe", lambda e, bl=bl, h=h: e.matmul(
                                pSg[bl][:, h, :], lhsT=Km[0:NS, bl, h * 128:(h + 1) * 128], rhs=vb_[0:NS, h * 128:(h + 1) * 128],
                                start=True, stop=True), reads=["qr1", "kr1", vbn], writes=[PPN[4 + bl]])
                    for bl in range(2):
                        for h in range(4):
                            op("dve", lambda e, bl=bl, h=h: e.scalar_tensor_tensor(
                                out=Sg_f[:, bl, h, :], in0=Sg_f[:, bl, h, :], scalar=GAM[h], in1=pSg[bl][:, h, :],
                                op0=ALU.mult, op1=ALU.add), reads=["hx", "gA", PPN[4 + bl]], writes=["hx", "gA"])
                    op("sp", lambda e, g2=g2: e.dma_start(
                        out=nrs[l, g2 * 2:(g2 + 1) * 2].rearrange("b h d e -> d b h e"), in_=Sg_f[:]),
                       reads=["hx", "gA"], dma_key="o_nrs")
                for h in range(4):
                    op("dve", lambda e, h=h: e.scalar_tensor_tensor(
                        out=o_s[0:NS, h * 128:(h + 1) * 128], in0=pp[h][0:NS, 0:128], scalar=GAM[h],
                        in1=o1[0:NS, h * 128:(h + 1) * 128], op0=ALU.mult, op1=ALU.add),
                       reads=[PPN[h], "cen"], writes=["gv"])
                osrc, osname = o_s[0:NS, :].rearrange("p (h d) -> p h d", h=4), "gv"
            if l == 0 and c == 15 and not sample:
                chk(300)
            if h1 is not None:
                h1()
            if l == 0 and c == 15 and not sample:
                chk(301)
            ps_s = pp[3]
            for g in range(4):
                if sample:
                    op("pe", lambda e, g=g: e.matmul(ps_s[0:NS, g * 128:(g + 1) * 128], lhsT=wsS_b[0:NS, g, :],
                                                      rhs=vn_[0:NS, g * 128:(g + 1) * 128], start=True, stop=True),
                       reads=["wsS_b", vnn], writes=[PPN[3]])
                else:
                    op("pe", lambda e, g=g: e.matmul(ps_s[:, g * 128:(g + 1) * 128], lhsT=wsT_b[:, g, :],
                                                      rhs=vn_[:, g * 128:(g + 1) * 128], start=True, stop=True),
                       reads=["wsT_b", vnn], writes=[PPN[3]])
            if l == 0 and c == 15 and not sample:
                chk(302)
            group_norm(osrc, osname, Pn, gn_tab, "gn_tab", cen[0:Pn, :], "cen", cen, ["cen"], t12, ["t1", "t2"], 4, "sttb")
            op("dve", lambda e: e.tensor_tensor(out=mixin[0:Pn, 0:512], in0=cen[0:Pn, :], in1=sg_[0:Pn, :], op=ALU.mult),
               reads=["cen", sgn], writes=["mixin"])
            if l == 0 and c == 15 and not sample:
                chk(303)
            for g in range(4):
                bcol = b00[0:NS, g:g + 1] if sample else bsT_t[:, l, g:g + 1]
                op("dve", lambda e, g=g, bcol=bcol: e.scalar_tensor_tensor(
                    out=mixin[0:Pn, 512 + g * 128:512 + (g + 1) * 128], in0=ps_s[0:Pn, g * 128:(g + 1) * 128],
                    scalar=bcol, in1=gu_[0:Pn, g * 128:(g + 1) * 128], op0=ALU.add, op1=ALU.mult),
                   reads=[PPN[3], gun, "b00", "bsT_t"], writes=["mixin"])
            if l == 0 and c == 15 and not sample:
                chk(304)
            for k in range(8):
                op("pe", lambda e, k=k: e.transpose(out=pT[:, k, 0:Pn], in_=mixin[0:Pn, k * 128:(k + 1) * 128],
                                                    identity=identb[0:Pn, 0:Pn]), reads=["mixin", "identb"], writes=["pT"])
            op("act", lambda e: e.copy(out=mixinT[:, :, 0:Pn], in_=pT[:, :, 0:Pn]), reads=["pT"], writes=["qkT"])
            obanks, onames = [pp[3], pp[5]], [PPN[3], PPN[5]]
            for n in range(2):
                for k in range(8):
                    op("pe", lambda e, n=n, k=k: e.matmul(obanks[n][0:Pn, :], lhsT=mixinT[:, k, 0:Pn],
                                                           rhs=Wout[:, k, n * 512:(n + 1) * 512], start=(k == 0), stop=(k == 7)),
                       reads=["qkT", "w_out"], writes=[onames[n]])
            if l == 0 and c == 15 and not sample:
                chk(305)
            if h2 is not None:
                h2()
            resid_update(xc, xname, Pn, obanks, onames, sample)

        def resid_update(xc, xname, Pn, banks2, names2, sample):
            gt = GTp
            gname = "GTp"
            for n in range(2):
                op("dve", lambda e, n=n: e.tensor_tensor(out=tmp[0:Pn, n * 512:(n + 1) * 512], in0=banks2[n][0:Pn, :],
                                                         in1=gt[0:Pn, n * 512:(n + 1) * 512], op=ALU.mult),
                   reads=[names2[n], gname], writes=["tmp"])
            op("pool", lambda e: e.tensor_tensor(out=xc, in0=xc, in1=tmp[0:Pn, :], op=ALU.add),
               reads=[xname, "tmp"], writes=[xname])

        def ffn_front(l, t):
            for cc in range(2):
                c = 2 * t + cc
                norm_mod(xres[:, c, :], "x%d" % c, 128, l, 3, 4, False, hdst=hT2[:, :, cc * 128:(cc + 1) * 128], hname="hT%d" % cc, sc=2 * cc)

        def ffn_tile(l, t, sample, mid=None):
            N = NS if sample else 256
            if sample:
                norm_mod(xs_t[0:NS, :], "xs_t", NS, l, 3, 4, True)
            for j in range(16):
                par = 0 if sample else j % 2
                for s_, m in enumerate((j, j + 16)):
                    pb = pp[(2 * j + s_) % 4]
                    pbn = PPN[(2 * j + s_) % 4]
                    for k in range(8):
                        op("pe", lambda e, m=m, k=k, pb=pb: e.matmul(pb[:, 0:N], lhsT=Wup[:, k, m * 128:(m + 1) * 128],
                                                                      rhs=hT2[:, k, 0:N], start=(k == 0), stop=(k == 7)),
                           reads=["w_up", "hT0", "hT1"], writes=[pbn])
                    un = "U%d%d" % (par, s_)
                    tn = "tb%d%d" % (par, s_)
                    tbv = tb[:, par, s_, 0:N]
                    op("act", lambda e, m=m, pb=pb, tbv=tbv: e.activation(
                        out=tbv, in_=pb[:, 0:N], func=AF.Identity, scale=cw[:, l, m, 2:3], bias=cb[:, l, m:m + 1]),
                       reads=[pbn, "cw", "cb"], writes=[tn])
                    if sample:
                        op("act", lambda e, m=m, pb=pb: e.copy(out=upTs[:, m, :], in_=pb[:, 0:NS]),
                           reads=[pbn], writes=["tmp"])
                        x1, x0 = scT1[:, m, :], scT0[:, m, :]
                        rd = ["U10", "U11", "tb10", "tb11"]
                    else:
                        Uv = U[:, par, s_, :]
                        op("pool", lambda e, m=m, Uv=Uv: e.tensor_copy(out=Uv[:, 0:2], in_=hist[:, m, :]),
                           reads=["hist"], writes=[un])
                        op("act", lambda e, pb=pb, Uv=Uv: e.copy(out=Uv[:, 2:258], in_=pb[:, 0:256]),
                           reads=[pbn], writes=[un])
                        op("pool", lambda e, m=m, Uv=Uv: e.tensor_copy(out=hist[:, m, :], in_=Uv[:, 256:258]),
                           reads=[un], writes=["hist"])
                        x1, x0 = Uv[:, 1:257], Uv[:, 0:256]
                        rd = [un]
                    op("dve", lambda e, m=m, x1=x1, tbv=tbv: e.scalar_tensor_tensor(
                        out=tbv, in0=x1, scalar=cw[:, l, m, 1:2], in1=tbv, op0=ALU.mult, op1=ALU.add),
                       reads=rd + ["cw", tn], writes=[tn])
                    op("dve", lambda e, m=m, x0=x0, tbv=tbv: e.scalar_tensor_tensor(
                        out=tbv, in0=x0, scalar=cw[:, l, m, 0:1], in1=tbv, op0=ALU.mult, op1=ALU.add),
                       reads=rd + ["cw", tn], writes=[tn])
                sn = "sl%d" % par
                op("act", lambda e, par=par: e.activation(out=sl[:, par, 0:N], in_=tb[:, par, 0, 0:N], func=AF.Silu),
                   reads=["tb%d0" % par], writes=[sn])
                op("dve", lambda e, j=j, par=par: e.tensor_tensor(out=fT[:, j, 0:N], in0=sl[:, par, 0:N], in1=tb[:, par, 1, 0:N],
                                                                  op=ALU.mult), reads=[sn, "tb%d1" % par], writes=["fT"])
            if mid is not None:
                mid()
            if sample:
                op("sp", lambda e: e.dma_start(out=ncsT[:, l, 1, :, :], in_=upTs), reads=["tmp"], dma_key="o_ncs")
                banks_seq = [([pp[4], pp[5]], [PPN[4], PPN[5]])]
            else:
                banks_seq = [([pp[4], pp[5]], [PPN[4], PPN[5]]), ([bank[1], pp[4]], ["pT2", PPN[4]])]
            for cc, (bks, bnames) in enumerate(banks_seq):
                Pn = NS if sample else 128
                for n in range(2):
                    for j in range(16):
                        op("pe", lambda e, n=n, j=j, cc=cc, bks=bks, Pn=Pn: e.matmul(
                            bks[n][0:Pn, :], lhsT=fT[:, j, cc * 128:cc * 128 + Pn],
                            rhs=Wdn[:, j, n * 512:(n + 1) * 512], start=(j == 0), stop=(j == 15)),
                           reads=["fT", "w_dn"], writes=[bnames[n]])
                if sample:
                    resid_update(xs_t[0:NS, :], "xs_t", NS, bks, bnames, True)
                else:
                    c = 2 * t + cc
                    resid_update(xres[:, c, :], "x%d" % c, 128, bks, bnames, False)

        def load_w(dst, src3, name, extra, nsplit):
            K = dst.shape[1]
            N = dst.shape[2]
            step = N // nsplit
            for i in range(nsplit):
                op("pool", lambda e, i=i: e.dma_start(
                    out=dst[:, :, i * step:(i + 1) * step],
                    in_=src3[:, i * step:(i + 1) * step].rearrange("(k p) n -> p k n", p=128)),
                   writes=[name] + extra, dma_key=name)

        try:
          chk(0)
          for l in range(2):
                op("dve", lambda e: e.memset(stt[:, 32:33], 0.0), writes=["w_dn"] + AU_NAMES + ["fence"])
                load_w(Win, w_in[l], "w_in", ["wa0", "wa1", "w_up"], 3)
                load_w(Wout, w_out[l], "w_out", ["wa0", "wa1", "w_up"], 1)

                def ld1(dst, src, name):
                    op("sp", lambda e: e.dma_start(out=dst, in_=src), writes=[name] + PH2_NAMES, dma_key="lp")

                ld1(gn_tab[:], gng[l:l + 1, :].partition_broadcast(128), "gn_tab")
                ld1(ln_tab[:], lng[l:l + 1, :].partition_broadcast(128), "ln_tab")
                ld1(ropes_t[0:NS, :], ropes, "ropes_t")
                ld1(dmaskT, dmaskT_d, "dmaskT")
                ld1(qdecT, qdecT_d, "qdecT")
                ld1(trilT, trilT_d, "trilT")
                ld1(delta16, delta16_d, "delta16")
                op("sp", lambda e, l=l: e.dma_start(out=tmp[:, 0:512].rearrange("p (g t) -> p g t", g=4), in_=wsT[l]),
                   writes=["tmp"], dma_key="lp")
                op("sp", lambda e, l=l: e.dma_start(out=w00[:], in_=ws00[l:l + 1, :].partition_broadcast(128)),
                   writes=["w00"], dma_key="lp")
                op("sp", lambda e, l=l: e.dma_start(out=b00[:], in_=bs0[l:l + 1, :].partition_broadcast(128)),
                   writes=["b00"], dma_key="lp")
                op("dve", lambda e: e.tensor_tensor(out=wsT_b, in0=tmp[:, 0:512].rearrange("p (g t) -> p g t", g=4),
                                                    in1=trilT.unsqueeze(1).to_broadcast([128, 4, 128]),
                                                    op=ALU.mult), reads=["tmp", "trilT"], writes=["wsT_b"] + PH2_NAMES)
                for g in range(4):
                    op("dve", lambda e, g=g: e.tensor_scalar(out=wsS_b[0:NS, g, :], in0=identf[0:NS, 0:NS], scalar1=w00[0:NS, g:g + 1],
                                                             scalar2=None, op0=ALU.mult), reads=["identf", "w00"], writes=["wsS_b"])
                chk(10 * l + 1)
                build_gt(l, 2, False)
                op("dve", lambda e: e.memset(S_f, 0.0), writes=["S_f"] + PH2_NAMES)
                op("dve", lambda e: e.memset(S_b, 0.0), writes=["S_b"] + PH2_NAMES)
                chk(10 * l + 2)
                mix_N(l, 0, False)
                mix_PA(l, 0, False)
                mix_EA(l, 0, False)
                mix_PB(l, 0, False)
                NCHd = int(_os.environ.get("MK_NCH", str(NCH)))
                DBG["lastc"] = NCHd - 1
                for c in range(NCHd):
                    last = (c + 1 == NCHd)
                    if not last:
                        mix_N(l, c + 1, False)

                    def h1(l=l, c=c, last=last):
                        mix_EB(l, c, False)
                        if not last:
                            mix_PA(l, c + 1, False)

                    def h2(l=l, c=c, last=last):
                        if not last:
                            mix_EA(l, c + 1, False)
                            mix_PB(l, c + 1, False)

                    mix_back(l, c, False, h1=h1, h2=h2)
                    chk(10 * l + 3)
                    if l == 0:
                        chk(200 + c)
                op("sp", lambda e, l=l: e.dma_start(out=nrp[l], in_=S_f), reads=["S_f"], dma_key="o_nrp")
                chk(10 * l + 4)
                build_gt(l, 2, True)
                mix_N(l, 0, True)
                mix_PA(l, 0, True)
                mix_EA(l, 0, True)
                mix_PB(l, 0, True)
                mix_EB(l, 0, True)
                mix_back(l, 0, True)
                chk(10 * l + 5)
                load_w(Wup, w_up[l], "w_up", ["w_in", "w_out"], 4)
                load_w(Wdn, w_down[l], "w_dn", AU_NAMES, 2)
                op("dve", lambda e: e.memset(stt[:, 33:34], 0.0), writes=PH1_NAMES + PH2_NAMES + ["fence2"])
                build_gt(l, 5, False)
                op("pool", lambda e: e.memset(hist[:], 0.0), writes=["hist"])
                chk(10 * l + 6)
                ffn_front(l, 0)
                for t in range(NCH // 2):
                    ffn_tile(l, t, False, mid=(lambda l=l, t=t: ffn_front(l, t + 1)) if t + 1 < NCH // 2 else None)
                    chk(10 * l + 7)
                op("sp", lambda e, l=l: e.dma_start(out=ncpT[:, l, :, :], in_=hist[:]), reads=["hist"], dma_key="o_ncp")
                chk(10 * l + 8)
                op("sp", lambda e, l=l: e.dma_start(out=scT0, in_=sconvT[:, l, 0]), writes=["U10", "U11"], dma_key="lp")
                op("sp", lambda e, l=l: e.dma_start(out=scT1, in_=sconvT[:, l, 1]), writes=["tb10", "tb11"], dma_key="lp")
                build_gt(l, 5, True)
                ffn_tile(l, 0, True)
                op("sp", lambda e, l=l: e.dma_start(out=ncsT[:, l, 0, :, :], in_=sconvT[:, l, 1, :, :]), dma_key="o_ncs0")

        except _Stop:
            P.emit()
            return nc

        op("sp", lambda e: e.dma_start(out=GTp[:], in_=gfin.partition_broadcast(128)), writes=["GTp"], dma_key="lp")

        def final_norm(xc, xname, Pn, dst, slot):
            slot = 0
            yo = tmp
            yn = "tmp"
            op("act", lambda e: e.activation(out=yo[0:Pn, :], in_=xc, func=AF.Square, scale=1.0 / 32.0,
                                             accum_out=stt[0:Pn, 40 + slot:41 + slot]),
               reads=[xname], writes=[yn, "stt%d" % slot])
            op("dve", lambda e: e.tensor_scalar(out=stt[0:Pn, 40 + slot:41 + slot], in0=stt[0:Pn, 40 + slot:41 + slot],
                                                scalar1=EPS, scalar2=None, op0=ALU.add),
               reads=["stt%d" % slot], writes=["stt%d" % slot])
            op("pool", lambda e: e.tensor_tensor(out=stt[0:Pn, 44 + slot:45 + slot], in0=stt[0:Pn, 40 + slot:41 + slot],
                                                 in1=mh[0:Pn, 0:1], op=ALU.pow), reads=["stt%d" % slot, "mh"], writes=["stt%d" % slot], keep=True)
            op("act", lambda e: e.activation(out=yo[0:Pn, :], in_=xc, func=AF.Copy, scale=stt[0:Pn, 44 + slot:45 + slot]),
               reads=[xname, "stt%d" % slot], writes=[yn])
            op("dve", lambda e: e.tensor_tensor(out=yo[0:Pn, :], in0=yo[0:Pn, :], in1=GTp[0:Pn, :], op=ALU.mult),
               reads=[yn, "GTp"], writes=[yn])
            op("sp", lambda e: e.dma_start(out=dst, in_=yo[0:Pn, :]), reads=[yn], dma_key="o_y%d" % slot)

        for c in range(NCH):
            final_norm(xres[:, c, :], "x%d" % c, 128, yp[c * 128:(c + 1) * 128, :], c % 2)
        final_norm(xs_t[0:NS, :], "xs_t", NS, ys, 0)

        P.emit()
    return nc


def _consts():
    half = 64
    freqs = np.exp(-math.log(10000.0) * np.arange(half, dtype=np.float32) / half).astype(np.float32)
    sc = np.float32(128.0 ** -0.5)

    def tab(pos):
        ang = pos.astype(np.float32)[:, None] * freqs[None, :]
        c = np.cos(ang).astype(np.float32)
        s = np.sin(ang).astype(np.float32)
        return np.concatenate([c, s, c * sc, s * sc], axis=1).astype(np.float32)

    ropep = tab(np.arange(T)).reshape(NCH, 128, 256)
    ropes = np.repeat(tab(np.array([16384])), NS, axis=0)
    lg = np.log1p(-np.exp2(-5.0 - np.arange(4, dtype=np.float32))).astype(np.float32)
    i = np.arange(128, dtype=np.float32)
    diff = i[:, None] - i[None, :]
    dmask = np.where(diff[None] >= 0, np.exp(np.maximum(diff, 0.0)[None] * lg[:, None, None]), 0.0).astype(np.float32)
    dmaskT = np.ascontiguousarray(dmask.transpose(2, 0, 1))
    q_dec = np.exp((i[:, None] + 1.0) * lg[None, :]).astype(np.float32)
    qdecT = np.ascontiguousarray(np.broadcast_to(q_dec.T[None], (128, 4, 128))).astype(np.float32)
    kdec = np.exp((127.0 - i)[:, None] * lg[None, :]).astype(np.float32)
    trilT = (i[:, None] <= i[None, :]).astype(np.float32)
    delta16 = np.ascontiguousarray(np.broadcast_to(np.eye(NS, dtype=np.float32)[None], (128, NS, NS)))
    deltaK = np.eye(NS, dtype=np.float32)
    return dict(identb=np.eye(128, dtype=np.float32).astype(ml_dtypes.bfloat16), identf=np.eye(128, dtype=np.float32),
                ropep=np.ascontiguousarray(ropep), ropes=np.ascontiguousarray(ropes), dmaskT=dmaskT, qdecT=qdecT,
                kdec=np.ascontiguousarray(kdec), trilT=np.ascontiguousarray(trilT), delta16=delta16, deltaK=deltaK)


_NC_CACHE = {}


def kernel(x_prompt, x_sample, state_ret, state_conv, c_prompt, c_sample, w_ada, b_ada, g_mix, w_in, ret_gn_gain,
           gmlp_ln_gain, w_s, b_s, w_out, g_ffn, w_up, conv_w, conv_b, w_down, g_final):
    f = lambda a: np.ascontiguousarray(np.asarray(a, dtype=np.float32))
    x_prompt, x_sample, state_ret, state_conv = f(x_prompt), f(x_sample), f(state_ret), f(state_conv)
    c_prompt, c_sample = f(c_prompt), f(c_sample)
    shared = dict(
        w_ada=f(w_ada), w_in=f(w_in), w_out=f(w_out), w_up=f(w_up), w_down=f(w_down),
        baT=f(f(b_ada).reshape(2, 6, 8, 128).transpose(3, 0, 1, 2)),
        gT=f(np.stack([f(g_mix)[0], f(g_mix)[1], f(g_ffn)[0], f(g_ffn)[1], f(g_final)]).reshape(5, 8, 128).transpose(2, 0, 1)),
        gfin=f(g_final).reshape(1, D), gng=f(ret_gn_gain), lng=f(gmlp_ln_gain),
        wsT=f(f(w_s).transpose(0, 3, 1, 2)),
        bsT=f(f(b_s).transpose(2, 0, 1)),
        ws00=f(f(w_s)[:, :, 0, 0]), bs0=f(f(b_s)[:, :, 0]),
        cwT=f(f(conv_w).reshape(2, 3, 32, 128).transpose(3, 0, 2, 1)),
        cbT=f(f(conv_b).reshape(2, 32, 128).transpose(2, 0, 1)),
    )
    shared.update(_consts())
    in_maps = []
    for c in range(NCORES):
        sl_ = slice(c * NS, (c + 1) * NS)
        m = dict(shared)
        m["xp"] = x_prompt[c]
        m["xs"] = f(x_sample[sl_, 0, :])
        m["cc"] = f(np.concatenate([c_prompt[c:c + 1], c_sample[sl_]], axis=0))
        m["sret"] = f(state_ret[:, sl_])
        m["sconvT"] = f(state_conv[:, sl_].reshape(2, NS, 2, 32, 128).transpose(4, 0, 2, 3, 1))
        in_maps.append(m)
    if "nc" not in _NC_CACHE:
        _NC_CACHE["nc"] = build_nc()
    import os
    ncr = int(os.environ.get("MK_CORES", str(NCORES)))
    res = run_bass_kernel_spmd(_NC_CACHE["nc"], in_maps[:ncr], core_ids=list(range(ncr)))
    R = list(res.results) + [res.results[0]] * (NCORES - ncr)
    y_prompt = np.stack([R[c]["yp"] for c in range(NCORES)]).astype(np.float32)
    y_sample = np.concatenate([R[c]["ys"] for c in range(NCORES)], axis=0).reshape(128, 1, D).astype(np.float32)
    new_ret_prompt = np.stack([R[c]["nrp"].transpose(0, 2, 1, 3) for c in range(NCORES)], axis=1).astype(np.float32)
    new_conv_prompt = np.stack([R[c]["ncpT"].transpose(1, 3, 2, 0).reshape(2, 2, 4096) for c in range(NCORES)], axis=1).astype(np.float32)
    new_ret_sample = np.concatenate([R[c]["nrs"] for c in range(NCORES)], axis=1).astype(np.float32)
    new_conv_sample = np.concatenate([R[c]["ncsT"].transpose(1, 4, 2, 3, 0).reshape(2, NS, 2, 4096) for c in range(NCORES)], axis=1).astype(np.float32)
    new_gmlp_v_sample = np.concatenate([R[c]["nvs"] for c in range(NCORES)], axis=1).reshape(2, 128, 1, 512).astype(np.float32)
    return (np.ascontiguousarray(y_prompt), np.ascontiguousarray(y_sample), np.ascontiguousarray(new_ret_prompt),
            np.ascontiguousarray(new_conv_prompt), np.ascontiguousarray(new_ret_sample),
            np.ascontiguousarray(new_conv_sample), np.ascontiguousarray(new_gmlp_v_sample))
```

```python
import contextlib
import math
import numpy as np
import ml_dtypes
import concourse.bass as bass
import concourse.mybir as mybir
from concourse.bass_utils import run_bass_kernel_spmd

F32 = mybir.dt.float32
BF16 = mybir.dt.bfloat16
AF = mybir.ActivationFunctionType
ALU = mybir.AluOpType
AX = mybir.AxisListType

ENGS = ("pe", "act", "dve", "pool", "sp")
NCORES = 8
T = 2048
NCH = 16
D = 1024
NS = 16
EPS = 1e-6
GAM = [1.0 - 2.0 ** (-5 - h) for h in range(4)]
import os as _os
POOL2DVE = _os.environ.get("MK_POOL2DVE", "0") == "1"


class Res:
    __slots__ = ("name", "writer", "readers")

    def __init__(self, name):
        self.name = name
        self.writer = None
        self.readers = []


class Op:
    __slots__ = ("eng", "fn", "dma_key", "deps", "signal", "ticket", "idx")

    def __init__(self, eng, fn, dma_key, idx):
        self.eng = eng
        self.fn = fn
        self.dma_key = dma_key
        self.deps = []
        self.signal = False
        self.ticket = None
        self.idx = idx


class Prog:
    def __init__(self, nc):
        self.nc = nc
        self.ops = []
        self.res = {}
        self.dma_count = {}

    def R(self, name):
        r = self.res.get(name)
        if r is None:
            r = Res(name)
            self.res[name] = r
        return r

    def op(self, eng, fn, reads=(), writes=(), dma_key=None, keep=False):
        if POOL2DVE and eng == "pool" and dma_key is None and not keep:
            eng = "dve"
        o = Op(eng, fn, dma_key, len(self.ops))
        deps = {}
        is_dma = dma_key is not None

        def add(d):
            if d is None or d is o:
                return
            deps[d.idx] = d

        rs = [self.R(r) for r in reads]
        ws = [self.R(w) for w in writes]
        for r in rs:
            add(r.writer)
        for w in ws:
            pw = w.writer
            if pw is not None:
                same_eng_compute = (pw.eng == eng == "pe" and pw.dma_key is None and not is_dma)
                same_key_dma = (is_dma and pw.dma_key == dma_key)
                if not (same_eng_compute or same_key_dma):
                    add(pw)
            for rd in w.readers:
                if rd.eng == eng == "pe" and rd.dma_key is None and not is_dma:
                    continue
                add(rd)
        for r in rs:
            r.readers.append(o)
        for w in ws:
            w.writer = o
            w.readers = []
        if is_dma:
            self.dma_count[dma_key] = self.dma_count.get(dma_key, 0) + 1
        latest = {}
        for d in deps.values():
            k = ("dma", d.dma_key) if d.dma_key is not None else ("eng", d.eng)
            if k not in latest or latest[k].idx < d.idx:
                latest[k] = d
        for d in latest.values():
            if d.dma_key is not None:
                o.deps.append((d, self.dma_count[d.dma_key] * 16))
            else:
                o.deps.append((d, None))
                d.signal = True
        self.ops.append(o)
        return o

    def emit(self):
        nc = self.nc
        counters = {e: 0 for e in ENGS}
        for o in self.ops:
            if o.dma_key is None and o.signal:
                counters[o.eng] += 1
                o.ticket = counters[o.eng]
        keys = sorted(self.dma_count.keys())
        self.stats = {e: (sum(1 for o in self.ops if o.eng == e), counters[e]) for e in ENGS}
        with contextlib.ExitStack() as st:
            esem = {e: st.enter_context(nc.semaphore("s_" + e)) for e in ENGS if e != "sp"}
            dsem = {k: st.enter_context(nc.semaphore("d_" + str(k))) for k in keys}
            block = st.enter_context(nc.Block())
            per_eng = {e: [o for o in self.ops if o.eng == e] for e in ENGS}

            def run(engname, engobj):
                waited = {}
                for o in per_eng[engname]:
                    need = {}
                    for d, val in o.deps:
                        if d.dma_key is not None:
                            s = dsem[d.dma_key]
                            v = val
                        else:
                            s = esem[d.eng]
                            v = d.ticket
                        key = id(s)
                        if v > waited.get(key, 0) and v > need.get(key, (None, 0))[1]:
                            need[key] = (s, v)
                    for key, (s, v) in need.items():
                        engobj.wait_ge(s, v)
                        waited[key] = v
                    ins = o.fn(engobj)
                    if o.dma_key is not None:
                        ins.then_inc(dsem[o.dma_key], 16)
                    elif o.signal:
                        ins.then_inc(esem[o.eng], 1)
                if engname == "sp":
                    for k in keys:
                        engobj.wait_ge(dsem[k], self.dma_count[k] * 16)

            @block.sync
            def _(e):
                run("sp", e)

            @block.scalar
            def _(e):
                run("act", e)

            @block.vector
            def _(e):
                run("dve", e)

            @block.gpsimd
            def _(e):
                run("pool", e)

            @block.tensor
            def _(e):
                run("pe", e)


def build_nc():
    nc = bass.Bass("TRN2", target_bir_lowering=False)

    def din(name, shape, dt=F32):
        return nc.dram_tensor(name, list(shape), dt, kind="ExternalInput").ap()

    def dout(name, shape, dt=F32):
        return nc.dram_tensor(name, list(shape), dt, kind="ExternalOutput").ap()

    xp = din("xp", [T, D])
    xs = din("xs", [NS, D])
    cc = din("cc", [NS + 1, D])
    sret = din("sret", [2, NS, 4, 128, 128])
    sconvT = din("sconvT", [128, 2, 2, 32, NS])
    w_ada = din("w_ada", [2, D, 6 * D])
    w_in = din("w_in", [2, D, 3072])
    w_out = din("w_out", [2, D, D])
    w_up = din("w_up", [2, D, 4096])
    w_down = din("w_down", [2, 2048, D])
    baT = din("baT", [128, 2, 6, 8])
    gT = din("gT", [128, 5, 8])
    gfin = din("gfin", [1, D])
    gng = din("gng", [2, 512])
    lng = din("lng", [2, 512])
    wsT = din("wsT", [2, 128, 4, 128])
    bsT = din("bsT", [128, 2, 4])
    ws00 = din("ws00", [2, 4])
    bs0 = din("bs0", [2, 4])
    cwT = din("cwT", [128, 2, 32, 3])
    cbT = din("cbT", [128, 2, 32])
    identb_d = din("identb", [128, 128], BF16)
    identf_d = din("identf", [128, 128])
    ropep = din("ropep", [NCH, 128, 256])
    ropes = din("ropes", [NS, 256])
    dmaskT_d = din("dmaskT", [128, 4, 128])
    qdecT_d = din("qdecT", [128, 4, 128])
    kdec_d = din("kdec", [128, 4])
    trilT_d = din("trilT", [128, 128])
    delta16_d = din("delta16", [128, NS, NS])
    deltaK_d = din("deltaK", [NS, NS])

    yp = dout("yp", [T, D])
    ys = dout("ys", [NS, D])
    nrp = dout("nrp", [2, 128, 4, 128])
    ncpT = dout("ncpT", [128, 2, 32, 2])
    nrs = dout("nrs", [2, NS, 4, 128, 128])
    ncsT = dout("ncsT", [128, 2, 2, 32, NS])
    nvs = dout("nvs", [2, NS, 512])

    with contextlib.ExitStack() as st:
        def sb(name, shape, dt=F32):
            return st.enter_context(nc.sbuf_tensor(name, list(shape), dt))

        def psb(name):
            return st.enter_context(nc.psum_tensor(name, [128, 512], F32))

        P = Prog(nc)
        op = P.op

        xres = sb("xres", [128, NCH, D])
        arena = sb("arena", [128, 49152], BF16)
        xs_t = sb("xs_t", [128, D])
        xn = sb("xn", [128, D], BF16)
        tmp = sb("tmp", [128, D])
        hT2 = sb("hT2", [128, 8, 256], BF16)
        hT = hT2[:, :, 0:128]
        stt = sb("stt", [128, 64])
        PH = sb("PH", [128, 4624])
        hist = sb("hist", [128, 32, 2])
        modT = sb("modT", [128, 2, 6, 8, NS + 1])
        baT_t = sb("baT_t", [128, 2, 6, 8])
        gT_t = sb("gT_t", [128, 5, 8])
        GTp = sb("GTp", [128, D])
        cT = sb("cT", [128, 8, NS + 1], BF16)
        identb = sb("identb_t", [128, 128], BF16)
        identf = sb("identf_t", [128, 128])
        mh = sb("mh", [128, 4])
        epsT = sb("epsT", [128, 4])
        kdec = sb("kdec_t", [128, 4])
        deltaK = sb("deltaK_t", [128, NS])
        wsS_b = sb("wsS_b", [128, 4, NS], BF16)
        bsT_t = sb("bsT_t", [128, 2, 4])
        w00 = sb("w00", [128, 4])
        b00 = sb("b00", [128, 4])
        cw = sb("cw", [128, 2, 32, 3])
        cb = sb("cb", [128, 2, 32])

        def phv(off, n, dt=F32, pat=None, **kw):
            a = PH[:, off:off + n]
            if dt == BF16:
                a = a.bitcast(BF16)
            if pat:
                a = a.rearrange(pat, **kw)
            return a

        ropeb = phv(0, 512, F32, "p (s c) -> p s c", s=2)
        ropes_t = phv(512, 256)
        dmaskT = phv(768, 512, F32, "p (h t) -> p h t", h=4)
        qdecT = phv(1280, 512, F32, "p (h t) -> p h t", h=4)
        trilT = phv(1792, 128)
        delta16 = phv(1920, 256, F32, "p (b m) -> p b m", b=NS)
        gn_tab = phv(2176, 512)
        ln_tab = phv(2688, 512)
        wsT_b = phv(3200, 256, BF16, "p (g t) -> p g t", g=4)
        S_f = phv(3456, 512, F32, "p (h t) -> p h t", h=4)
        S_b = phv(3968, 256, BF16, "p (h t) -> p h t", h=4)
        fT = phv(0, 2048, BF16, "p (j t) -> p j t", j=16)
        U = phv(2048, 1040, F32, "p (a s t) -> p a s t", a=2, s=2)
        tb = phv(3088, 1024, F32, "p (a s t) -> p a s t", a=2, s=2)
        sl = phv(4112, 512, F32, "p (a t) -> p a t", a=2)
        scT0 = phv(2048 + 520, 512, F32, "p (m b) -> p m b", m=32)
        scT1 = phv(3088 + 512, 512, F32, "p (m b) -> p m b", m=32)
        upTs = tmp[:, 0:512].rearrange("p (m b) -> p m b", m=32)
        PH1_NAMES = ["ropeb0", "ropeb1", "ropes_t", "dmaskT", "qdecT", "trilT", "delta16", "gn_tab", "ln_tab", "wsT_b", "S_f", "S_b", "q2T"]
        PH2_NAMES = ["fT", "U00", "U01", "U10", "U11", "tb00", "tb01", "tb10", "tb11", "sl0", "sl1"]

        bank = [psb("bank%d" % i) for i in range(8)]
        pT = bank[0][:, 0:512].bitcast(BF16).rearrange("p (k t) -> p k t", k=8)
        pT2 = bank[1][:, 0:512].bitcast(BF16).rearrange("p (k t) -> p k t", k=8)
        pTf = [bank[0], bank[1]]
        pp = bank[2:8]
        PPN = ["pp%d" % i for i in range(6)]

        def aview(off, n, dt, pat=None, **kw):
            a = arena[:, off:off + n]
            if dt == F32:
                a = a.bitcast(F32)
            if pat:
                a = a.rearrange(pat, **kw)
            return a

        wa = [aview(0, 8192, BF16, "p (k n) -> p k n", k=8), aview(8192, 8192, BF16, "p (k n) -> p k n", k=8)]
        Win = aview(0, 24576, BF16, "p (k n) -> p k n", k=8)
        Wout = aview(24576, 8192, BF16, "p (k n) -> p k n", k=8)
        Wup = aview(0, 32768, BF16, "p (k n) -> p k n", k=8)
        Wdn = aview(32768, 16384, BF16, "p (k n) -> p k n", k=16)
        AU = 32768
        qr = [aview(AU + 0, 512, BF16), aview(AU + 1536, 512, BF16)]
        kr = [aview(AU + 512, 512, BF16), aview(AU + 2048, 512, BF16)]
        vb = [aview(AU + 1024, 512, BF16), aview(AU + 2560, 512, BF16)]
        kk = aview(AU + 3072, 512, BF16)
        sg = [aview(AU + 3584, 1024, F32), aview(AU + 4608, 1024, F32)]
        gu = [aview(AU + 5632, 1024, F32), aview(AU + 6656, 1024, F32)]
        gv = aview(AU + 7680, 1024, F32)
        vn = [aview(AU + 8704, 512, BF16), aview(AU + 9216, 512, BF16)]
        t1 = aview(AU + 9728, 512, F32)
        t2 = aview(AU + 10240, 512, F32)
        t12 = aview(AU + 9728, 1024, F32)
        hx = aview(AU + 10752, 1024, F32)
        gA = aview(AU + 11776, 1024, F32)
        cen = aview(AU + 12800, 1024, F32)
        qkT = aview(AU + 13824, 1024, BF16, "p (k t) -> p k t", k=8)
        mixinT = qkT
        mixin = aview(AU + 14848, 1024, BF16)
        attTm = aview(AU + 15872, 512, BF16, "p (k t) -> p k t", k=4)
        q2T = phv(4224, 256, BF16, "p (k t) -> p k t", k=4)
        vnf = sg[1]
        Sg_f = aview(AU + 10752, 2048, F32, "p (b h e) -> p b h e", b=2, h=4)
        Sg_b = aview(AU + 6656, 1024, BF16, "p (b h e) -> p b h e", b=2, h=4)
        Km = aview(AU + 1536, 1024, BF16, "p (b n) -> p b n", b=2)
        o1 = cen
        o_s = gv
        Qm = aview(AU + 14848, 1024, BF16, "p (h b m) -> p h b m", h=4, b=NS)
        AU_NAMES = ["qr0", "kr0", "vb0", "qr1", "kr1", "vb1", "kk", "sg0", "sg1", "gu0", "gu1", "gv", "vn0", "vn1",
                    "t1", "t2", "hx", "gA", "cen", "qkT", "mixin", "attTm"]

        def ld(dst, src, name, eng="sp", key="c"):
            op(eng, lambda e: e.dma_start(out=dst, in_=src), writes=[name], dma_key=key)

        ld(identb[:], identb_d, "identb")
        ld(identf[:], identf_d, "identf")
        ld(tmp[0:NS + 1, :], cc, "tmp")
        ld(baT_t[:], baT, "baT_t")
        ld(gT_t[:], gT, "gT_t")
        ld(xs_t[0:NS, :], xs, "xs_t")
        ld(kdec[:], kdec_d, "kdec")
        ld(deltaK[0:NS, :], deltaK_d, "deltaK")
        ld(bsT_t[:], bsT, "bsT_t")
        ld(cw[:], cwT, "cw")
        ld(cb[:], cbT, "cb")
        op("pool", lambda e: e.memset(mh[:], -0.5), writes=["mh"])
        op("pool", lambda e: e.memset(epsT[:], EPS), writes=["epsT"])
        for q4 in range(4):
            op("act", lambda e, q4=q4: e.dma_start(
                out=xres[:, q4 * 4:(q4 + 1) * 4, :],
                in_=xp[q4 * 512:(q4 + 1) * 512, :].rearrange("(c p) f -> p c f", p=128)),
               writes=["x%d" % c for c in range(q4 * 4, q4 * 4 + 4)], dma_key="x%d" % q4)

        op("act", lambda e: e.activation(out=xn[0:NS + 1, :], in_=tmp[0:NS + 1, :], func=AF.Silu),
           reads=["tmp"], writes=["xn"])
        for k in range(8):
            op("pe", lambda e, k=k: e.transpose(out=pT[:, k, 0:NS + 1], in_=xn[0:NS + 1, k * 128:(k + 1) * 128],
                                                identity=identb[0:NS + 1, 0:NS + 1]),
               reads=["xn", "identb"], writes=["pT"])
        op("dve", lambda e: e.tensor_copy(out=cT[:], in_=pT[:, :, 0:NS + 1]), reads=["pT"], writes=["cT"])
        ji = 0
        for l in range(2):
            for v in range(6):
                b = ji % 2
                wname = "wa%d" % b
                op("pool", lambda e, l=l, v=v, b=b: e.dma_start(
                    out=wa[b], in_=w_ada[l, :, v * D:(v + 1) * D].rearrange("(k p) n -> p k n", p=128)),
                   writes=[wname], dma_key=wname)
                pbank = pp[ji % 2]
                pname = PPN[ji % 2]
                pv = pbank[:, 0:8 * (NS + 1)].rearrange("p (m t) -> p m t", m=8)
                for m in range(8):
                    for k in range(8):
                        op("pe", lambda e, b=b, m=m, k=k, pv=pv: e.matmul(
                            pv[:, m, :], lhsT=wa[b][:, k, m * 128:(m + 1) * 128], rhs=cT[:, k, :],
                            start=(k == 0), stop=(k == 7)),
                           reads=[wname, "cT"], writes=[pname])
                op("dve", lambda e, l=l, v=v, pv=pv: e.tensor_tensor(
                    out=modT[:, l, v, :, :], in0=pv,
                    in1=baT_t[:, l, v, :].unsqueeze(2).to_broadcast([128, 8, NS + 1]), op=ALU.add),
                   reads=[pname, "baT_t"], writes=["modT"])
                if v in (1, 4):
                    gi = l if v == 1 else 2 + l
                    op("dve", lambda e, l=l, v=v, gi=gi: e.scalar_tensor_tensor(
                        out=modT[:, l, v, :, :], in0=modT[:, l, v, :, :], scalar=1.0,
                        in1=gT_t[:, gi, :].unsqueeze(2).to_broadcast([128, 8, NS + 1]),
                        op0=ALU.add, op1=ALU.mult),
                       reads=["modT", "gT_t"], writes=["modT"])
                ji += 1

        import os
        STOP = int(os.environ.get("MK_STOP", "-1"))

        class _Stop(Exception):
            pass

        def chk(n):
            if STOP == n:
                raise _Stop()

        def build_gt(l, v, sample):
            if not sample:
                op("act", lambda e: e.copy(
                    out=tmp[:].rearrange("p (k t) -> p k t", k=8),
                    in_=modT[:, l, v, :, 0:1].to_broadcast([128, 8, 128])),
                   reads=["modT"], writes=["tmp"])
                for k in range(8):
                    bk = pTf[k // 4]
                    op("pe", lambda e, k=k, bk=bk: e.transpose(
                        out=bk[:, (k % 4) * 128:(k % 4 + 1) * 128], in_=tmp[:, k * 128:(k + 1) * 128], identity=identf[:]),
                       reads=["tmp", "identf"], writes=["pT" if k < 4 else "pT2"])
                op("act", lambda e: e.copy(out=GTp[:, 0:512], in_=pTf[0][:]), reads=["pT"], writes=["GTp"])
                op("act", lambda e: e.copy(out=GTp[:, 512:1024], in_=pTf[1][:]), reads=["pT2"], writes=["GTp"])
            else:
                for k in range(8):
                    bk = pTf[k // 4]
                    op("pe", lambda e, k=k, bk=bk: e.transpose(
                        out=bk[0:NS, (k % 4) * 128:(k % 4 + 1) * 128], in_=modT[:, l, v, k, 1:NS + 1], identity=identf[:]),
                       reads=["modT", "identf"], writes=["pT" if k < 4 else "pT2"])
                op("act", lambda e: e.copy(out=GTp[0:NS, 0:512], in_=pTf[0][0:NS, :]), reads=["pT"], writes=["GTp"])
                op("act", lambda e: e.copy(out=GTp[0:NS, 512:1024], in_=pTf[1][0:NS, :]), reads=["pT2"], writes=["GTp"])

        def norm_mod(xc, xname, Pn, l, vsh, vsc, sample, hdst=None, hname="hT0", sc=0):
            if hdst is None:
                hdst = hT
            op("act", lambda e: e.activation(out=xn[0:Pn, :], in_=xc, func=AF.Square, scale=1.0 / 32.0,
                                             accum_out=stt[0:Pn, sc:sc + 1]),
               reads=[xname], writes=["xn", "sttn%d" % sc])
            op("dve", lambda e: e.tensor_tensor(out=stt[0:Pn, sc:sc + 1], in0=stt[0:Pn, sc:sc + 1], in1=epsT[0:Pn, 0:1],
                                                op=ALU.add), reads=["sttn%d" % sc, "epsT"], writes=["sttn%d" % sc])
            op("pool", lambda e: e.tensor_tensor(out=stt[0:Pn, sc + 1:sc + 2], in0=stt[0:Pn, sc:sc + 1], in1=mh[0:Pn, 0:1], op=ALU.pow),
               reads=["sttn%d" % sc, "mh"], writes=["sttn%d" % sc], keep=True)
            op("act", lambda e: e.activation(out=xn[0:Pn, :], in_=xc, func=AF.Copy, scale=stt[0:Pn, sc + 1:sc + 2]),
               reads=[xname, "sttn%d" % sc], writes=["xn"])
            for k in range(8):
                op("pe", lambda e, k=k: e.transpose(out=pT[:, k, 0:Pn], in_=xn[0:Pn, k * 128:(k + 1) * 128],
                                                    identity=identb[0:Pn, 0:Pn]),
                   reads=["xn", "identb"], writes=["pT"])
            if sample:
                gtab = modT[:, l, vsc, :, 1:NS + 1]
                stab = modT[:, l, vsh, :, 1:NS + 1]
                t3 = tmp[:].rearrange("p (k t) -> p k t", k=8)[:, :, 0:Pn]
                op("dve", lambda e: e.tensor_tensor(out=t3, in0=pT[:, :, 0:Pn], in1=gtab, op=ALU.mult),
                   reads=["pT", "modT"], writes=["tmp"])
                op("pool", lambda e: e.tensor_tensor(out=hdst[:, :, 0:Pn], in0=t3, in1=stab, op=ALU.add),
                   reads=["tmp", "modT"], writes=[hname])
            else:
                for k in range(8):
                    op("act", lambda e, k=k: e.activation(out=hdst[:, k, 0:Pn], in_=pT[:, k, 0:Pn], func=AF.Identity,
                                                          scale=modT[:, l, vsc, k, 0:1], bias=modT[:, l, vsh, k, 0:1]),
                       reads=["pT", "modT"], writes=[hname])

        def group_norm(src, srcname, Pn, gain, gname, out, oname, cbuf, cnames, sbuf_, snames, col0, stn):
            m = stt[0:Pn, col0:col0 + 4]
            vv_ = stt[0:Pn, col0 + 4:col0 + 8]
            r = stt[0:Pn, col0 + 8:col0 + 12]
            cen3 = cbuf[0:Pn, :].rearrange("p (g c) -> p g c", g=4)
            sq3 = sbuf_[0:Pn, :].rearrange("p (g c) -> p g c", g=4)
            op("dve", lambda e: e.tensor_reduce(out=m, in_=src, axis=AX.X, op=ALU.add), reads=[srcname], writes=[stn])
            op("dve", lambda e: e.scalar_tensor_tensor(out=cen3, in0=m.unsqueeze(2).to_broadcast([Pn, 4, 128]),
                                                       scalar=-1.0 / 128.0, in1=src, op0=ALU.mult, op1=ALU.add),
               reads=[srcname, stn], writes=cnames)
            op("pool", lambda e: e.tensor_tensor(out=sq3, in0=cen3, in1=cen3, op=ALU.mult),
               reads=cnames, writes=snames)
            op("dve", lambda e: e.tensor_reduce(out=vv_, in_=sq3, axis=AX.X, op=ALU.add),
               reads=snames, writes=[stn])
            op("dve", lambda e: e.scalar_tensor_tensor(out=vv_, in0=vv_, scalar=1.0 / 128.0, in1=epsT[0:Pn, :],
                                                       op0=ALU.mult, op1=ALU.add), reads=[stn, "epsT"], writes=[stn])
            op("pool", lambda e: e.tensor_tensor(out=r, in0=vv_, in1=mh[0:Pn, :], op=ALU.pow),
               reads=[stn, "mh"], writes=[stn], keep=True)
            op("dve", lambda e: e.tensor_tensor(out=cen3, in0=cen3, in1=r.unsqueeze(2).to_broadcast([Pn, 4, 128]),
                                                op=ALU.mult), reads=cnames + [stn], writes=cnames)
            op("pool", lambda e: e.tensor_tensor(out=out, in0=cbuf[0:Pn, :], in1=gain[0:Pn, :], op=ALU.mult),
               reads=cnames + [gname], writes=[oname])

        def gelu(src, srcname, Pn, out, oname):
            op("act", lambda e: e.activation(out=hx[0:Pn, :], in_=src, func=AF.Copy, scale=0.5),
               reads=[srcname], writes=["hx"])
            op("act", lambda e: e.activation(out=gA[0:Pn, :], in_=src, func=AF.Square), reads=[srcname], writes=["gA"])
            op("dve", lambda e: e.scalar_tensor_tensor(out=gA[0:Pn, :], in0=gA[0:Pn, :], scalar=0.044715, in1=hx[0:Pn, :],
                                                       op0=ALU.mult, op1=ALU.mult), reads=["gA", "hx"], writes=["gA"])
            op("dve", lambda e: e.tensor_tensor(out=gA[0:Pn, :], in0=gA[0:Pn, :], in1=hx[0:Pn, :], op=ALU.add),
               reads=["gA", "hx"], writes=["gA"])
            op("act", lambda e: e.activation(out=gA[0:Pn, :], in_=gA[0:Pn, :], func=AF.Tanh,
                                             scale=2.0 * math.sqrt(2.0 / math.pi)), reads=["gA"], writes=["gA"])
            op("dve", lambda e: e.scalar_tensor_tensor(out=out, in0=gA[0:Pn, :], scalar=1.0, in1=hx[0:Pn, :],
                                                       op0=ALU.add, op1=ALU.mult), reads=["gA", "hx"], writes=[oname])

        def rope(src, srcname, Pn, tab, tabname, coff, out, oname):
            s3 = src.rearrange("p (h d) -> p h d", h=4)
            x1 = s3[:, :, 0:64]
            x2 = s3[:, :, 64:128]
            cosb = tab[:, coff:coff + 64].unsqueeze(1).to_broadcast([Pn, 4, 64])
            sinb = tab[:, coff + 64:coff + 128].unsqueeze(1).to_broadcast([Pn, 4, 64])
            a = t1[0:Pn, :].rearrange("p (h d) -> p h d", h=4)
            b_ = t2[0:Pn, :].rearrange("p (h d) -> p h d", h=4)
            o3 = out.rearrange("p (h d) -> p h d", h=4)
            op("dve", lambda e: e.tensor_tensor(out=a, in0=x1, in1=cosb, op=ALU.mult), reads=[srcname, tabname], writes=["t1"])
            op("dve", lambda e: e.tensor_tensor(out=b_, in0=x2, in1=sinb, op=ALU.mult), reads=[srcname, tabname], writes=["t2"])
            op("pool", lambda e: e.tensor_tensor(out=o3[:, :, 0:64], in0=a, in1=b_, op=ALU.subtract),
               reads=["t1", "t2"], writes=[oname])
            op("dve", lambda e: e.tensor_tensor(out=a, in0=x1, in1=sinb, op=ALU.mult), reads=[srcname, tabname, oname], writes=["t1"])
            op("dve", lambda e: e.tensor_tensor(out=b_, in0=x2, in1=cosb, op=ALU.mult), reads=[srcname, tabname, oname], writes=["t2"])
            op("pool", lambda e: e.tensor_tensor(out=o3[:, :, 64:128], in0=a, in1=b_, op=ALU.add),
               reads=["t1", "t2"], writes=[oname])

        DBG = {}

        def mix_ctx(c, sample):
            par = 0 if sample else c % 2
            Pn = NS if sample else 128
            if sample:
                return par, Pn, xs_t[0:NS, :], "xs_t", ropes_t[0:NS, :], "ropes_t"
            return par, Pn, xres[:, c, :], "x%d" % c, ropeb[:, c % 2, :], "ropeb%d" % (c % 2)

        def mix_N(l, c, sample):
            par, Pn, xc, xname, rtab, rname = mix_ctx(c, sample)
            if not sample:
                op("sp", lambda e: e.dma_start(out=rtab, in_=ropep[c]),
                   writes=[rname] + (PH2_NAMES if c < 2 else []), dma_key=rname)
            norm_mod(xc, xname, Pn, l, 0, 1, sample, hdst=hT2[:, :, par * 128:(par + 1) * 128], hname="hT%d" % par, sc=2 * par)

        def mix_PA(l, c, sample):
            par, Pn, xc, xname, rtab, rname = mix_ctx(c, sample)
            for n in range(3):
                for k in range(8):
                    op("pe", lambda e, n=n, k=k: e.matmul(pp[n][0:Pn, :], lhsT=hT2[:, k, par * 128:par * 128 + Pn],
                                                           rhs=Win[:, k, n * 512:(n + 1) * 512],
                                                           start=(k == 0), stop=(k == 7)),
                       reads=["hT%d" % par, "w_in"], writes=[PPN[n]])

        def mix_EA(l, c, sample):
            par, Pn, xc, xname, rtab, rname = mix_ctx(c, sample)
            rope(pp[0][0:Pn, :], PPN[0], Pn, rtab, rname, 0, qr[par][0:Pn, :], "qr%d" % par)
            rope(pp[1][0:Pn, :], PPN[1], Pn, rtab, rname, 128, kr[par][0:Pn, :], "kr%d" % par)
            op("act", lambda e: e.copy(out=vb[par][0:Pn, :], in_=pp[2][0:Pn, :]), reads=[PPN[2]], writes=["vb%d" % par])

        def mix_PB(l, c, sample):
            par, Pn, xc, xname, rtab, rname = mix_ctx(c, sample)
            for n in range(3):
                for k in range(8):
                    op("pe", lambda e, n=n, k=k: e.matmul(pp[n][0:Pn, :], lhsT=hT2[:, k, par * 128:par * 128 + Pn],
                                                           rhs=Win[:, k, (3 + n) * 512:(4 + n) * 512],
                                                           start=(k == 0), stop=(k == 7)),
                       reads=["hT%d" % par, "w_in"], writes=[PPN[n]])

        def mix_EB(l, c, sample):
            par, Pn, xc, xname, rtab, rname = mix_ctx(c, sample)
            op("act", lambda e: e.activation(out=sg[par][0:Pn, :], in_=pp[0][0:Pn, :], func=AF.Silu),
               reads=[PPN[0]], writes=["sg%d" % par])
            gelu(pp[1][0:Pn, :], PPN[1], Pn, gu[par][0:Pn, :], "gu%d" % par)
            gelu(pp[2][0:Pn, :], PPN[2], Pn, gv[0:Pn, :], "gv")
            gv3 = gv[0:Pn, :].rearrange("p (g c) -> p g c", g=4)
            if sample:
                group_norm(gv3, "gv", Pn, ln_tab, "ln_tab", vnf[0:Pn, :], "sg1", hx, ["hx"], gA, ["gA"], 16, "sttf")
                op("act", lambda e: e.copy(out=vn[0][0:Pn, :], in_=vnf[0:Pn, :]), reads=["sg1"], writes=["vn0"])
                op("sp", lambda e: e.dma_start(out=nvs[l], in_=vnf[0:NS, :]), reads=["sg1"], dma_key="o_nvs")
            else:
                group_norm(gv3, "gv", Pn, ln_tab, "ln_tab", vn[par][0:Pn, :], "vn%d" % par, hx, ["hx"], gA, ["gA"], 16, "sttf")

        def mix_back(l, c, sample, h1=None, h2=None):
            par, Pn, xc, xname, rtab, rname = mix_ctx(c, sample)
            qrn, krn, vbn, sgn, gun, vnn = ["%s%d" % (b_, par) for b_ in ("qr", "kr", "vb", "sg", "gu", "vn")]
            qr_, kr_, vb_, sg_, gu_, vn_ = qr[par], kr[par], vb[par], sg[par], gu[par], vn[par]
            if not sample:
                op("dve", lambda e: e.tensor_tensor(out=kk[:].rearrange("p (h d) -> p h d", h=4),
                                                    in0=kr_[:].rearrange("p (h d) -> p h d", h=4),
                                                    in1=kdec[:].unsqueeze(2).to_broadcast([128, 4, 128]), op=ALU.mult),
                   reads=[krn, "kdec"], writes=["kk"])
                if l == 0 and c == DBG.get('lastc') and not sample:
                    chk(310)
                for h in range(4):
                    op("pe", lambda e, h=h: e.transpose(out=pT2[:, h, :], in_=qr_[:, h * 128:(h + 1) * 128], identity=identb[:]),
                       reads=[qrn, "identb"], writes=["pT2"])
                for h in range(4):
                    op("pe", lambda e, h=h: e.transpose(out=pT2[:, 4 + h, :], in_=kr_[:, h * 128:(h + 1) * 128], identity=identb[:]),
                       reads=[krn, "identb"], writes=["pT2"])
                if l == 0 and c == DBG.get('lastc') and not sample:
                    chk(311)
                op("act", lambda e: e.copy(out=qkT[:], in_=pT2[:]), reads=["pT2"], writes=["qkT"])
                op("dve", lambda e: e.tensor_tensor(out=q2T, in0=qkT[:, 0:4, :], in1=qdecT, op=ALU.mult),
                   reads=["qkT", "qdecT"], writes=["q2T"])
                if l == 0 and c == DBG.get('lastc') and not sample:
                    chk(312)
                pa = pp[3][:].rearrange("p (h t) -> p h t", h=4)
                for h in range(4):
                    op("pe", lambda e, h=h: e.matmul(pa[:, h, :], lhsT=qkT[:, 4 + h, :], rhs=qkT[:, h, :], start=True, stop=True),
                       reads=["qkT"], writes=[PPN[3]])
                if l == 0 and c == DBG.get('lastc') and not sample:
                    chk(313)
                op("dve", lambda e: e.tensor_tensor(out=attTm[:], in0=pa, in1=dmaskT, op=ALU.mult),
                   reads=[PPN[3], "dmaskT"], writes=["attTm"])
                if l == 0 and c == DBG.get('lastc') and not sample:
                    chk(314)
                po = pp[4][:].rearrange("p (h t) -> p h t", h=4)
                for h in range(4):
                    op("pe", lambda e, h=h: e.matmul(po[:, h, :], lhsT=attTm[:, h, :], rhs=vb_[:, h * 128:(h + 1) * 128],
                                                      start=True, stop=False), reads=["attTm", vbn], writes=[PPN[4]])
                    op("pe", lambda e, h=h: e.matmul(po[:, h, :], lhsT=q2T[:, h, :], rhs=S_b[:, h, :],
                                                      start=False, stop=True), reads=["q2T", "S_b"], writes=[PPN[4]])
                if l == 0 and c == DBG.get('lastc') and not sample:
                    chk(315)
                pS = pp[5][:].rearrange("p (h t) -> p h t", h=4)
                for h in range(4):
                    op("pe", lambda e, h=h: e.matmul(pS[:, h, :], lhsT=kk[:, h * 128:(h + 1) * 128], rhs=vb_[:, h * 128:(h + 1) * 128],
                                                      start=True, stop=True), reads=["kk", vbn], writes=[PPN[5]])
                for h in range(4):
                    op("dve", lambda e, h=h: e.scalar_tensor_tensor(out=S_f[:, h, :], in0=S_f[:, h, :], scalar=GAM[h] ** 128,
                                                                    in1=pS[:, h, :], op0=ALU.mult, op1=ALU.add),
                       reads=["S_f", PPN[5]], writes=["S_f"])
                op("act", lambda e: e.copy(out=S_b, in_=S_f), reads=["S_f"], writes=["S_b"])
                osrc, osname = po, PPN[4]
            else:
                pr3 = t12[0:NS, :]
                op("dve", lambda e: e.tensor_tensor(out=pr3, in0=qr_[0:NS, :], in1=kr_[0:NS, :], op=ALU.mult),
                   reads=[qrn, krn], writes=["t1", "t2"])
                op("dve", lambda e: e.tensor_reduce(out=stt[0:NS, 28:32], in_=pr3.rearrange("p (h d) -> p h d", h=4),
                                                    axis=AX.X, op=ALU.add), reads=["t1", "t2"], writes=["sttq"])
                op("dve", lambda e: e.tensor_tensor(out=o1[0:NS, :].rearrange("p (h d) -> p h d", h=4),
                                                    in0=vb_[0:NS, :].rearrange("p (h d) -> p h d", h=4),
                                                    in1=stt[0:NS, 28:32].unsqueeze(2).to_broadcast([NS, 4, 128]), op=ALU.mult),
                   reads=[vbn, "sttq"], writes=["cen"])
                for h in range(4):
                    op("pe", lambda e, h=h: e.transpose(out=pT2[:, h, 0:NS], in_=qr_[0:NS, h * 128:(h + 1) * 128],
                                                        identity=identb[0:NS, 0:NS]), reads=[qrn, "identb"], writes=["pT2"])
                op("act", lambda e: e.copy(out=qkT[:, 0:4, 0:NS], in_=pT2[:, 0:4, 0:NS]), reads=["pT2"], writes=["qkT"])
                op("dve", lambda e: e.tensor_tensor(
                    out=Qm[:], in0=qkT[:, 0:4, 0:NS].unsqueeze(3).to_broadcast([128, 4, NS, NS]),
                    in1=delta16.unsqueeze(1).to_broadcast([128, 4, NS, NS]), op=ALU.mult),
                   reads=["qkT", "delta16"], writes=["mixin"])
                for g2 in range(NS // 2):
                    op("sp", lambda e, g2=g2: e.dma_start(
                        out=Sg_f[:], in_=sret[l, g2 * 2:(g2 + 1) * 2].rearrange("b h d e -> d b h e")),
                       writes=["hx", "gA"], dma_key="sgf")
                    op("pool", lambda e, g2=g2: e.dma_start(
                        out=Sg_b[:], in_=sret[l, g2 * 2:(g2 + 1) * 2].rearrange("b h d e -> d b h e")),
                       writes=["gu1"], dma_key="sgb")
                    op("dve", lambda e, g2=g2: e.tensor_tensor(
                        out=Km[0:NS, :, :], in0=kr_[0:NS, :].unsqueeze(1).to_broadcast([NS, 2, 512]),
                        in1=deltaK[0:NS, g2 * 2:(g2 + 1) * 2].unsqueeze(2).to_broadcast([NS, 2, 512]), op=ALU.mult),
                       reads=[krn, "deltaK"], writes=["qr1", "kr1"])
                    for bl in range(2):
                        bq = g2 * 2 + bl
                        for h in range(4):
                            op("pe", lambda e, bl=bl, bq=bq, h=h: e.matmul(
                                pp[h][0:NS, 0:128], lhsT=Qm[:, h, bq, :], rhs=Sg_b[:, bl, h, :],
                                start=(bq == 0), stop=(bq == NS - 1)),
                               reads=["mixin", "gu1"], writes=[PPN[h]])
                    pSg = [pp[4][:].rearrange("p (h t) -> p h t", h=4), pp[5][:].rearrange("p (h t) -> p h t", h=4)]
                    for bl in range(2):
                        for h in range(4):
                            op("pe", lambda e, bl=bl, h=h: e.matmul(
                                pSg[bl][:, h, :], lhsT=Km[0:NS, bl, h * 128:(h + 1) * 128], rhs=vb_[0:NS, h * 128:(h + 1) * 128],
                                start=True, stop=True), reads=["qr1", "kr1", vbn], writes=[PPN[4 + bl]])
                    for bl in range(2):
                        for h in range(4):
                            op("dve", lambda e, bl=bl, h=h: e.scalar_tensor_tensor(
                                out=Sg_f[:, bl, h, :], in0=Sg_f[:, bl, h, :], scalar=GAM[h], in1=pSg[bl][:, h, :],
                                op0=ALU.mult, op1=ALU.add), reads=["hx", "gA", PPN[4 + bl]], writes=["hx", "gA"])
                    op("sp", lambda e, g2=g2: e.dma_start(
                        out=nrs[l, g2 * 2:(g2 + 1) * 2].rearrange("b h d e -> d b h e"), in_=Sg_f[:]),
                       reads=["hx", "gA"], dma_key="o_nrs")
                for h in range(4):
                    op("dve", lambda e, h=h: e.scalar_tensor_tensor(
                        out=o_s[0:NS, h * 128:(h + 1) * 128], in0=pp[h][0:NS, 0:128], scalar=GAM[h],
                        in1=o1[0:NS, h * 128:(h + 1) * 128], op0=ALU.mult, op1=ALU.add),
                       reads=[PPN[h], "cen"], writes=["gv"])
                osrc, osname = o_s[0:NS, :].rearrange("p (h d) -> p h d", h=4), "gv"
            if l == 0 and c == 15 and not sample:
                chk(300)
            if h1 is not None:
                h1()
            if l == 0 and c == 15 and not sample:
                chk(301)
            ps_s = pp[3]
            for g in range(4):
                if sample:
                    op("pe", lambda e, g=g: e.matmul(ps_s[0:NS, g * 128:(g + 1) * 128], lhsT=wsS_b[0:NS, g, :],
                                                      rhs=vn_[0:NS, g * 128:(g + 1) * 128], start=True, stop=True),
                       reads=["wsS_b", vnn], writes=[PPN[3]])
                else:
                    op("pe", lambda e, g=g: e.matmul(ps_s[:, g * 128:(g + 1) * 128], lhsT=wsT_b[:, g, :],
                                                      rhs=vn_[:, g * 128:(g + 1) * 128], start=True, stop=True),
                       reads=["wsT_b", vnn], writes=[PPN[3]])
            if l == 0 and c == 15 and not sample:
                chk(302)
            group_norm(osrc, osname, Pn, gn_tab, "gn_tab", cen[0:Pn, :], "cen", cen, ["cen"], t12, ["t1", "t2"], 4, "sttb")
            op("dve", lambda e: e.tensor_tensor(out=mixin[0:Pn, 0:512], in0=cen[0:Pn, :], in1=sg_[0:Pn, :], op=ALU.mult),
               reads=["cen", sgn], writes=["mixin"])
            if l == 0 and c == 15 and not sample:
                chk(303)
            for g in range(4):
                bcol = b00[0:NS, g:g + 1] if sample else bsT_t[:, l, g:g + 1]
                op("dve", lambda e, g=g, bcol=bcol: e.scalar_tensor_tensor(
                    out=mixin[0:Pn, 512 + g * 128:512 + (g + 1) * 128], in0=ps_s[0:Pn, g * 128:(g + 1) * 128],
                    scalar=bcol, in1=gu_[0:Pn, g * 128:(g + 1) * 128], op0=ALU.add, op1=ALU.mult),
                   reads=[PPN[3], gun, "b00", "bsT_t"], writes=["mixin"])
            if l == 0 and c == 15 and not sample:
                chk(304)
            for k in range(8):
                op("pe", lambda e, k=k: e.transpose(out=pT[:, k, 0:Pn], in_=mixin[0:Pn, k * 128:(k + 1) * 128],
                                                    identity=identb[0:Pn, 0:Pn]), reads=["mixin", "identb"], writes=["pT"])
            op("act", lambda e: e.copy(out=mixinT[:, :, 0:Pn], in_=pT[:, :, 0:Pn]), reads=["pT"], writes=["qkT"])
            obanks, onames = [pp[3], pp[5]], [PPN[3], PPN[5]]
            for n in range(2):
                for k in range(8):
                    op("pe", lambda e, n=n, k=k: e.matmul(obanks[n][0:Pn, :], lhsT=mixinT[:, k, 0:Pn],
                                                           rhs=Wout[:, k, n * 512:(n + 1) * 512], start=(k == 0), stop=(k == 7)),
                       reads=["qkT", "w_out"], writes=[onames[n]])
            if l == 0 and c == 15 and not sample:
                chk(305)
            if h2 is not None:
                h2()
            resid_update(xc, xname, Pn, obanks, onames, sample)

        def resid_update(xc, xname, Pn, banks2, names2, sample):
            gt = GTp
            gname = "GTp"
            for n in range(2):
                op("dve", lambda e, n=n: e.tensor_tensor(out=tmp[0:Pn, n * 512:(n + 1) * 512], in0=banks2[n][0:Pn, :],
                                                         in1=gt[0:Pn, n * 512:(n + 1) * 512], op=ALU.mult),
                   reads=[names2[n], gname], writes=["tmp"])
            op("pool", lambda e: e.tensor_tensor(out=xc, in0=xc, in1=tmp[0:Pn, :], op=ALU.add),
               reads=[xname, "tmp"], writes=[xname])

        def ffn_front(l, t):
            for cc in range(2):
                c = 2 * t + cc
                norm_mod(xres[:, c, :], "x%d" % c, 128, l, 3, 4, False, hdst=hT2[:, :, cc * 128:(cc + 1) * 128], hname="hT%d" % cc, sc=2 * cc)

        def ffn_tile(l, t, sample, mid=None):
            N = NS if sample else 256
            if sample:
                norm_mod(xs_t[0:NS, :], "xs_t", NS, l, 3, 4, True)
            for j in range(16):
                par = 0 if sample else j % 2
                for s_, m in enumerate((j, j + 16)):
                    pb = pp[(2 * j + s_) % 4]
                    pbn = PPN[(2 * j + s_) % 4]
                    for k in range(8):
                        op("pe", lambda e, m=m, k=k, pb=pb: e.matmul(pb[:, 0:N], lhsT=Wup[:, k, m * 128:(m + 1) * 128],
                                                                      rhs=hT2[:, k, 0:N], start=(k == 0), stop=(k == 7)),
                           reads=["w_up", "hT0", "hT1"], writes=[pbn])
                    un = "U%d%d" % (par, s_)
                    tn = "tb%d%d" % (par, s_)
                    tbv = tb[:, par, s_, 0:N]
                    op("act", lambda e, m=m, pb=pb, tbv=tbv: e.activation(
                        out=tbv, in_=pb[:, 0:N], func=AF.Identity, scale=cw[:, l, m, 2:3], bias=cb[:, l, m:m + 1]),
                       reads=[pbn, "cw", "cb"], writes=[tn])
                    if sample:
                        op("act", lambda e, m=m, pb=pb: e.copy(out=upTs[:, m, :], in_=pb[:, 0:NS]),
                           reads=[pbn], writes=["tmp"])
                        x1, x0 = scT1[:, m, :], scT0[:, m, :]
                        rd = ["U10", "U11", "tb10", "tb11"]
                    else:
                        Uv = U[:, par, s_, :]
                        op("pool", lambda e, m=m, Uv=Uv: e.tensor_copy(out=Uv[:, 0:2], in_=hist[:, m, :]),
                           reads=["hist"], writes=[un])
                        op("act", lambda e, pb=pb, Uv=Uv: e.copy(out=Uv[:, 2:258], in_=pb[:, 0:256]),
                           reads=[pbn], writes=[un])
                        op("pool", lambda e, m=m, Uv=Uv: e.tensor_copy(out=hist[:, m, :], in_=Uv[:, 256:258]),
                           reads=[un], writes=["hist"])
                        x1, x0 = Uv[:, 1:257], Uv[:, 0:256]
                        rd = [un]
                    op("dve", lambda e, m=m, x1=x1, tbv=tbv: e.scalar_tensor_tensor(
                        out=tbv, in0=x1, scalar=cw[:, l, m, 1:2], in1=tbv, op0=ALU.mult, op1=ALU.add),
                       reads=rd + ["cw", tn], writes=[tn])
                    op("dve", lambda e, m=m, x0=x0, tbv=tbv: e.scalar_tensor_tensor(
                        out=tbv, in0=x0, scalar=cw[:, l, m, 0:1], in1=tbv, op0=ALU.mult, op1=ALU.add),
                       reads=rd + ["cw", tn], writes=[tn])
                sn = "sl%d" % par
                op("act", lambda e, par=par: e.activation(out=sl[:, par, 0:N], in_=tb[:, par, 0, 0:N], func=AF.Silu),
                   reads=["tb%d0" % par], writes=[sn])
                op("dve", lambda e, j=j, par=par: e.tensor_tensor(out=fT[:, j, 0:N], in0=sl[:, par, 0:N], in1=tb[:, par, 1, 0:N],
                                                                  op=ALU.mult), reads=[sn, "tb%d1" % par], writes=["fT"])
            if mid is not None:
                mid()
            if sample:
                op("sp", lambda e: e.dma_start(out=ncsT[:, l, 1, :, :], in_=upTs), reads=["tmp"], dma_key="o_ncs")
                banks_seq = [([pp[4], pp[5]], [PPN[4], PPN[5]])]
            else:
                banks_seq = [([pp[4], pp[5]], [PPN[4], PPN[5]]), ([bank[1], pp[4]], ["pT2", PPN[4]])]
            for cc, (bks, bnames) in enumerate(banks_seq):
                Pn = NS if sample else 128
                for n in range(2):
                    for j in range(16):
                        op("pe", lambda e, n=n, j=j, cc=cc, bks=bks, Pn=Pn: e.matmul(
                            bks[n][0:Pn, :], lhsT=fT[:, j, cc * 128:cc * 128 + Pn],
                            rhs=Wdn[:, j, n * 512:(n + 1) * 512], start=(j == 0), stop=(j == 15)),
                           reads=["fT", "w_dn"], writes=[bnames[n]])
                if sample:
                    resid_update(xs_t[0:NS, :], "xs_t", NS, bks, bnames, True)
                else:
                    c = 2 * t + cc
                    resid_update(xres[:, c, :], "x%d" % c, 128, bks, bnames, False)

        def load_w(dst, src3, name, extra, nsplit):
            K = dst.shape[1]
            N = dst.shape[2]
            step = N // nsplit
            for i in range(nsplit):
                op("pool", lambda e, i=i: e.dma_start(
                    out=dst[:, :, i * step:(i + 1) * step],
                    in_=src3[:, i * step:(i + 1) * step].rearrange("(k p) n -> p k n", p=128)),
                   writes=[name] + extra, dma_key=name)

        try:
          chk(0)
          for l in range(2):
                op("pool", lambda e: e.memset(stt[:, 32:33], 0.0), writes=["w_dn"] + AU_NAMES + ["fence"])
                load_w(Win, w_in[l], "w_in", ["wa0", "wa1", "w_up"], 3)
                load_w(Wout, w_out[l], "w_out", ["wa0", "wa1", "w_up"], 1)

                def ld1(dst, src, name):
                    op("sp", lambda e: e.dma_start(out=dst, in_=src), writes=[name] + PH2_NAMES, dma_key="lp")

                ld1(gn_tab[:], gng[l:l + 1, :].partition_broadcast(128), "gn_tab")
                ld1(ln_tab[:], lng[l:l + 1, :].partition_broadcast(128), "ln_tab")
                ld1(ropes_t[0:NS, :], ropes, "ropes_t")
                ld1(dmaskT, dmaskT_d, "dmaskT")
                ld1(qdecT, qdecT_d, "qdecT")
                ld1(trilT, trilT_d, "trilT")
                ld1(delta16, delta16_d, "delta16")
                op("sp", lambda e, l=l: e.dma_start(out=tmp[:, 0:512].rearrange("p (g t) -> p g t", g=4), in_=wsT[l]),
                   writes=["tmp"], dma_key="lp")
                op("sp", lambda e, l=l: e.dma_start(out=w00[:], in_=ws00[l:l + 1, :].partition_broadcast(128)),
                   writes=["w00"], dma_key="lp")
                op("sp", lambda e, l=l: e.dma_start(out=b00[:], in_=bs0[l:l + 1, :].partition_broadcast(128)),
                   writes=["b00"], dma_key="lp")
                op("dve", lambda e: e.tensor_tensor(out=wsT_b, in0=tmp[:, 0:512].rearrange("p (g t) -> p g t", g=4),
                                                    in1=trilT.unsqueeze(1).to_broadcast([128, 4, 128]),
                                                    op=ALU.mult), reads=["tmp", "trilT"], writes=["wsT_b"] + PH2_NAMES)
                for g in range(4):
                    op("dve", lambda e, g=g: e.tensor_tensor(out=wsS_b[0:NS, g, :], in0=identf[0:NS, 0:NS],
                                                             in1=w00[0:NS, g:g + 1].to_broadcast([NS, NS]), op=ALU.mult),
                       reads=["identf", "w00"], writes=["wsS_b"])
                chk(10 * l + 1)
                build_gt(l, 2, False)
                op("pool", lambda e: e.memset(S_f, 0.0), writes=["S_f"] + PH2_NAMES)
                op("pool", lambda e: e.memset(S_b, 0.0), writes=["S_b"] + PH2_NAMES)
                chk(10 * l + 2)
                mix_N(l, 0, False)
                mix_PA(l, 0, False)
                mix_EA(l, 0, False)
                mix_PB(l, 0, False)
                NCHd = int(_os.environ.get("MK_NCH", str(NCH)))
                DBG["lastc"] = NCHd - 1
                for c in range(NCHd):
                    last = (c + 1 == NCHd)
                    if not last:
                        mix_N(l, c + 1, False)

                    def h1(l=l, c=c, last=last):
                        mix_EB(l, c, False)
                        if not last:
                            mix_PA(l, c + 1, False)

                    def h2(l=l, c=c, last=last):
                        if not last:
                            mix_EA(l, c + 1, False)
                            mix_PB(l, c + 1, False)

                    mix_back(l, c, False, h1=h1, h2=h2)
                    chk(10 * l + 3)
                    if l == 0:
                        chk(200 + c)
                op("sp", lambda e, l=l: e.dma_start(out=nrp[l], in_=S_f), reads=["S_f"], dma_key="o_nrp")
                chk(10 * l + 4)
                build_gt(l, 2, True)
                mix_N(l, 0, True)
                mix_PA(l, 0, True)
                mix_EA(l, 0, True)
                mix_PB(l, 0, True)
                mix_EB(l, 0, True)
                mix_back(l, 0, True)
                chk(10 * l + 5)
                load_w(Wup, w_up[l], "w_up", ["w_in", "w_out"], 4)
                load_w(Wdn, w_down[l], "w_dn", AU_NAMES, 2)
                op("pool", lambda e: e.memset(stt[:, 33:34], 0.0), writes=PH1_NAMES + PH2_NAMES + ["fence2"])
                build_gt(l, 5, False)
                op("pool", lambda e: e.memset(hist[:], 0.0), writes=["hist"])
                chk(10 * l + 6)
                ffn_front(l, 0)
                for t in range(NCH // 2):
                    ffn_tile(l, t, False, mid=(lambda l=l, t=t: ffn_front(l, t + 1)) if t + 1 < NCH // 2 else None)
                    chk(10 * l + 7)
                op("sp", lambda e, l=l: e.dma_start(out=ncpT[:, l, :, :], in_=hist[:]), reads=["hist"], dma_key="o_ncp")
                chk(10 * l + 8)
                op("sp", lambda e, l=l: e.dma_start(out=scT0, in_=sconvT[:, l, 0]), writes=["U10", "U11"], dma_key="lp")
                op("sp", lambda e, l=l: e.dma_start(out=scT1, in_=sconvT[:, l, 1]), writes=["tb10", "tb11"], dma_key="lp")
                build_gt(l, 5, True)
                ffn_tile(l, 0, True)
                op("sp", lambda e, l=l: e.dma_start(out=ncsT[:, l, 0, :, :], in_=sconvT[:, l, 1, :, :]), dma_key="o_ncs0")

        except _Stop:
            P.emit()
            return nc

        op("sp", lambda e: e.dma_start(out=GTp[:], in_=gfin.partition_broadcast(128)), writes=["GTp"], dma_key="lp")

        def final_norm(xc, xname, Pn, dst, slot):
            slot = 0
            yo = tmp
            yn = "tmp"
            op("act", lambda e: e.activation(out=yo[0:Pn, :], in_=xc, func=AF.Square, scale=1.0 / 32.0,
                                             accum_out=stt[0:Pn, 40 + slot:41 + slot]),
               reads=[xname], writes=[yn, "stt%d" % slot])
            op("dve", lambda e: e.tensor_tensor(out=stt[0:Pn, 40 + slot:41 + slot], in0=stt[0:Pn, 40 + slot:41 + slot],
                                                in1=epsT[0:Pn, 0:1], op=ALU.add),
               reads=["stt%d" % slot, "epsT"], writes=["stt%d" % slot])
            op("pool", lambda e: e.tensor_tensor(out=stt[0:Pn, 44 + slot:45 + slot], in0=stt[0:Pn, 40 + slot:41 + slot],
                                                 in1=mh[0:Pn, 0:1], op=ALU.pow), reads=["stt%d" % slot, "mh"], writes=["stt%d" % slot], keep=True)
            op("act", lambda e: e.activation(out=yo[0:Pn, :], in_=xc, func=AF.Copy, scale=stt[0:Pn, 44 + slot:45 + slot]),
               reads=[xname, "stt%d" % slot], writes=[yn])
            op("dve", lambda e: e.tensor_tensor(out=yo[0:Pn, :], in0=yo[0:Pn, :], in1=GTp[0:Pn, :], op=ALU.mult),
               reads=[yn, "GTp"], writes=[yn])
            op("sp", lambda e: e.dma_start(out=dst, in_=yo[0:Pn, :]), reads=[yn], dma_key="o_y%d" % slot)

        for c in range(NCH):
            final_norm(xres[:, c, :], "x%d" % c, 128, yp[c * 128:(c + 1) * 128, :], c % 2)
        final_norm(xs_t[0:NS, :], "xs_t", NS, ys, 0)

        P.emit()
    return nc


def _consts():
    half = 64
    freqs = np.exp(-math.log(10000.0) * np.arange(half, dtype=np.float32) / half).astype(np.float32)
    sc = np.float32(128.0 ** -0.5)

    def tab(pos):
        ang = pos.astype(np.float32)[:, None] * freqs[None, :]
        c = np.cos(ang).astype(np.float32)
        s = np.sin(ang).astype(np.float32)
        return np.concatenate([c, s, c * sc, s * sc], axis=1).astype(np.float32)

    ropep = tab(np.arange(T)).reshape(NCH, 128, 256)
    ropes = np.repeat(tab(np.array([16384])), NS, axis=0)
    lg = np.log1p(-np.exp2(-5.0 - np.arange(4, dtype=np.float32))).astype(np.float32)
    i = np.arange(128, dtype=np.float32)
    diff = i[:, None] - i[None, :]
    dmask = np.where(diff[None] >= 0, np.exp(np.maximum(diff, 0.0)[None] * lg[:, None, None]), 0.0).astype(np.float32)
    dmaskT = np.ascontiguousarray(dmask.transpose(2, 0, 1))
    q_dec = np.exp((i[:, None] + 1.0) * lg[None, :]).astype(np.float32)
    qdecT = np.ascontiguousarray(np.broadcast_to(q_dec.T[None], (128, 4, 128))).astype(np.float32)
    kdec = np.exp((127.0 - i)[:, None] * lg[None, :]).astype(np.float32)
    trilT = (i[:, None] <= i[None, :]).astype(np.float32)
    delta16 = np.ascontiguousarray(np.broadcast_to(np.eye(NS, dtype=np.float32)[None], (128, NS, NS)))
    deltaK = np.eye(NS, dtype=np.float32)
    return dict(identb=np.eye(128, dtype=np.float32).astype(ml_dtypes.bfloat16), identf=np.eye(128, dtype=np.float32),
                ropep=np.ascontiguousarray(ropep), ropes=np.ascontiguousarray(ropes), dmaskT=dmaskT, qdecT=qdecT,
                kdec=np.ascontiguousarray(kdec), trilT=np.ascontiguousarray(trilT), delta16=delta16, deltaK=deltaK)


_NC_CACHE = {}


def kernel(x_prompt, x_sample, state_ret, state_conv, c_prompt, c_sample, w_ada, b_ada, g_mix, w_in, ret_gn_gain,
           gmlp_ln_gain, w_s, b_s, w_out, g_ffn, w_up, conv_w, conv_b, w_down, g_final):
    f = lambda a: np.ascontiguousarray(np.asarray(a, dtype=np.float32))
    x_prompt, x_sample, state_ret, state_conv = f(x_prompt), f(x_sample), f(state_ret), f(state_conv)
    c_prompt, c_sample = f(c_prompt), f(c_sample)
    shared = dict(
        w_ada=f(w_ada), w_in=f(w_in), w_out=f(w_out), w_up=f(w_up), w_down=f(w_down),
        baT=f(f(b_ada).reshape(2, 6, 8, 128).transpose(3, 0, 1, 2)),
        gT=f(np.stack([f(g_mix)[0], f(g_mix)[1], f(g_ffn)[0], f(g_ffn)[1], f(g_final)]).reshape(5, 8, 128).transpose(2, 0, 1)),
        gfin=f(g_final).reshape(1, D), gng=f(ret_gn_gain), lng=f(gmlp_ln_gain),
        wsT=f(f(w_s).transpose(0, 3, 1, 2)),
        bsT=f(f(b_s).transpose(2, 0, 1)),
        ws00=f(f(w_s)[:, :, 0, 0]), bs0=f(f(b_s)[:, :, 0]),
        cwT=f(f(conv_w).reshape(2, 3, 32, 128).transpose(3, 0, 2, 1)),
        cbT=f(f(conv_b).reshape(2, 32, 128).transpose(2, 0, 1)),
    )
    shared.update(_consts())
    in_maps = []
    for c in range(NCORES):
        sl_ = slice(c * NS, (c + 1) * NS)
        m = dict(shared)
        m["xp"] = x_prompt[c]
        m["xs"] = f(x_sample[sl_, 0, :])
        m["cc"] = f(np.concatenate([c_prompt[c:c + 1], c_sample[sl_]], axis=0))
        m["sret"] = f(state_ret[:, sl_])
        m["sconvT"] = f(state_conv[:, sl_].reshape(2, NS, 2, 32, 128).transpose(4, 0, 2, 3, 1))
        in_maps.append(m)
    if "nc" not in _NC_CACHE:
        _NC_CACHE["nc"] = build_nc()
    import os
    ncr = int(os.environ.get("MK_CORES", str(NCORES)))
    res = run_bass_kernel_spmd(_NC_CACHE["nc"], in_maps[:ncr], core_ids=list(range(ncr)))
    R = list(res.results) + [res.results[0]] * (NCORES - ncr)
    y_prompt = np.stack([R[c]["yp"] for c in range(NCORES)]).astype(np.float32)
    y_sample = np.concatenate([R[c]["ys"] for c in range(NCORES)], axis=0).reshape(128, 1, D).astype(np.float32)
    new_ret_prompt = np.stack([R[c]["nrp"].transpose(0, 2, 1, 3) for c in range(NCORES)], axis=1).astype(np.float32)
    new_conv_prompt = np.stack([R[c]["ncpT"].transpose(1, 3, 2, 0).reshape(2, 2, 4096) for c in range(NCORES)], axis=1).astype(np.float32)
    new_ret_sample = np.concatenate([R[c]["nrs"] for c in range(NCORES)], axis=1).astype(np.float32)
    new_conv_sample = np.concatenate([R[c]["ncsT"].transpose(1, 4, 2, 3, 0).reshape(2, NS, 2, 4096) for c in range(NCORES)], axis=1).astype(np.float32)
    new_gmlp_v_sample = np.concatenate([R[c]["nvs"] for c in range(NCORES)], axis=1).reshape(2, 128, 1, 512).astype(np.float32)
    return (np.ascontiguousarray(y_prompt), np.ascontiguousarray(y_sample), np.ascontiguousarray(new_ret_prompt),
            np.ascontiguousarray(new_conv_prompt), np.ascontiguousarray(new_ret_sample),
            np.ascontiguousarray(new_conv_sample), np.ascontiguousarray(new_gmlp_v_sample))
```
